# Optimizing a Trainium2 kernel written in Bass

```python
import math
import jax
import jax.numpy as jnp
from jax import lax
import numpy as np

D_MODEL = 1024
BATCH = 16
SEQ = 4096
DEPTH = 2

HEAD_DIM = 64
SB_HEADS = D_MODEL // (2 * HEAD_DIM)
MOBA_HEADS = D_MODEL // (2 * HEAD_DIM)
SB_WIDTH = SB_HEADS * HEAD_DIM
MOBA_WIDTH = MOBA_HEADS * HEAD_DIM
ATT_IN_WIDTH = 3 * SB_WIDTH + 3 * MOBA_WIDTH + SB_WIDTH + MOBA_WIDTH
ATT_OUT_WIDTH = SB_WIDTH + MOBA_WIDTH
SB_BLOCK = 128
MOBA_BLOCK = 256
MOBA_TOPK = 3
MOBA_Q_CHUNK = 16
ROPE_THETA = 500000.0
ROT_DIM = HEAD_DIM // 4
LRU_WIDTH = D_MODEL
LRU_BLOCKS = 8
LRU_BLOCK_WIDTH = LRU_WIDTH // LRU_BLOCKS
CONV_WIDTH = 4
LRU_C = 8.0
N_ATT_LAYERS = (DEPTH + 1) // 2
N_LRU_LAYERS = DEPTH // 2
EPS = 1e-6

kernel_name = "hybrid_sb_moba_rglru_adaln"


def rmsnorm(x, g):
    x32 = x.astype(jnp.float32)
    xn = x32 * lax.rsqrt(jnp.mean(x32 * x32, axis=-1, keepdims=True) + EPS)
    return xn * g.astype(jnp.float32)


def rope_partial(t, positions):
    half = ROT_DIM // 2
    inv = ROPE_THETA ** (-jnp.arange(0, ROT_DIM, 2, dtype=jnp.float32) / ROT_DIM)
    ang = positions.astype(jnp.float32)[:, None, :, None] * inv
    cos, sin = jnp.cos(ang), jnp.sin(ang)
    tr = t[..., :ROT_DIM].astype(jnp.float32)
    t1, t2 = tr[..., :half], tr[..., half:]
    rot = jnp.concatenate([t1 * cos - t2 * sin, t2 * cos + t1 * sin], axis=-1)
    return jnp.concatenate([rot.astype(t.dtype), t[..., ROT_DIM:]], axis=-1)


def stick_breaking_attention(q, k, v):
    S = q.shape[2]
    scale = HEAD_DIM ** -0.5
    outs = []
    for blk in range(S // SB_BLOCK):
        t0, t1 = blk * SB_BLOCK, (blk + 1) * SB_BLOCK
        qb = q[:, :, t0:t1].astype(jnp.float32)
        kb = k[:, :, :t1].astype(jnp.float32)
        vb = v[:, :, :t1].astype(jnp.float32)
        z = jnp.einsum('bhqd,bhkd->bhqk', qb, kb) * scale
        mask = jnp.arange(t1)[None, :] < jnp.arange(t0, t1)[:, None]
        log_1m = jnp.where(mask, jax.nn.log_sigmoid(-z), 0.0)
        later = lax.cumsum(log_1m, axis=3, reverse=True) - log_1m
        w = jnp.where(mask, jnp.exp(jax.nn.log_sigmoid(z) + later), 0.0)
        outs.append(jnp.einsum('bhqk,bhkd->bhqd', w, vb))
    return jnp.concatenate(outs, axis=2).astype(q.dtype)


def moba_attention(q, k, v):
    B, H, S, dh = q.shape
    nb = -(-S // MOBA_BLOCK)
    pad = nb * MOBA_BLOCK - S
    kp = jnp.pad(k, ((0, 0), (0, 0), (0, pad), (0, 0)))
    vp = jnp.pad(v, ((0, 0), (0, 0), (0, pad), (0, 0)))
    kb = kp.reshape(B, H, nb, MOBA_BLOCK, dh)
    vb = vp.reshape(B, H, nb, MOBA_BLOCK, dh)
    kmean = jnp.mean(kb.astype(jnp.float32), axis=3)
    gscore = jnp.einsum('bhsd,bhnd->bhsn', q.astype(jnp.float32), kmean)
    qblk = jnp.arange(S) // MOBA_BLOCK
    past = jnp.arange(nb)[None, :] < qblk[:, None]
    gscore = jnp.where(past, gscore, -jnp.inf)
    n_sel = min(MOBA_TOPK, nb)
    _, sel = lax.top_k(gscore, n_sel)
    sel_valid = sel < qblk[:, None]
    bi = jnp.arange(B)[:, None, None, None]
    hi = jnp.arange(H)[None, :, None, None]
    scale = dh ** -0.5

    def chunk(ci):
        t0 = ci * MOBA_Q_CHUNK
        qc = lax.dynamic_slice_in_dim(q, t0, MOBA_Q_CHUNK, axis=2).astype(jnp.float32)
        sel_c = lax.dynamic_slice_in_dim(sel, t0, MOBA_Q_CHUNK, axis=2)
        val_c = lax.dynamic_slice_in_dim(sel_valid, t0, MOBA_Q_CHUNK, axis=2)
        k_sel = kb[bi, hi, sel_c].astype(jnp.float32)
        v_sel = vb[bi, hi, sel_c].astype(jnp.float32)
        s_sel = jnp.einsum('bhqd,bhqnkd->bhqnk', qc, k_sel) * scale
        s_sel = jnp.where(val_c[..., None], s_sel, -jnp.inf)
        s_sel = s_sel.reshape(B, H, MOBA_Q_CHUNK, n_sel * MOBA_BLOCK)
        ob = t0 // MOBA_BLOCK
        k_own = lax.dynamic_index_in_dim(kb, ob, axis=2, keepdims=False).astype(jnp.float32)
        v_own = lax.dynamic_index_in_dim(vb, ob, axis=2, keepdims=False).astype(jnp.float32)
        s_own = jnp.einsum('bhqd,bhkd->bhqk', qc, k_own) * scale
        qpos = t0 + jnp.arange(MOBA_Q_CHUNK)
        kpos = ob * MOBA_BLOCK + jnp.arange(MOBA_BLOCK)
        s_own = jnp.where(kpos[None, :] <= qpos[:, None], s_own, -jnp.inf)
        p = jax.nn.softmax(jnp.concatenate([s_sel, s_own], axis=-1), axis=-1)
        p_sel = p[..., :n_sel * MOBA_BLOCK].reshape(B, H, MOBA_Q_CHUNK, n_sel, MOBA_BLOCK)
        p_own = p[..., n_sel * MOBA_BLOCK:]
        return (jnp.einsum('bhqnk,bhqnkd->bhqd', p_sel, v_sel)
                + jnp.einsum('bhqk,bhkd->bhqd', p_own, v_own))

    out = lax.map(chunk, jnp.arange(S // MOBA_Q_CHUNK))
    out = out.transpose(1, 2, 0, 3, 4).reshape(B, H, S, dh)
    return out.astype(q.dtype)


def attention_layer(h, positions, w_in, w_out):
    B, S, _ = h.shape
    u = h @ w_in
    cuts = np.cumsum([SB_WIDTH] * 3 + [MOBA_WIDTH] * 3 + [SB_WIDTH])
    q_a, k_a, v_a, q_b, k_b, v_b, g_a, g_b = jnp.split(u, list(cuts), axis=-1)

    def heads(t, n):
        return t.reshape(B, S, n, HEAD_DIM).transpose(0, 2, 1, 3)

    def merge(t):
        return t.transpose(0, 2, 1, 3).reshape(B, S, -1)

    o_a = stick_breaking_attention(heads(q_a, SB_HEADS), heads(k_a, SB_HEADS), heads(v_a, SB_HEADS))
    qr = rope_partial(heads(q_b, MOBA_HEADS), positions)
    kr = rope_partial(heads(k_b, MOBA_HEADS), positions)
    o_b = moba_attention(qr, kr, heads(v_b, MOBA_HEADS))
    y = jnp.concatenate([merge(o_a) * jax.nn.silu(g_a), merge(o_b) * jax.nn.silu(g_b)], axis=-1)
    return y @ w_out


def rglru_layer(h, w_in, conv_w, conv_b, w_a, b_a, w_x, b_x, lam, w_out):
    B, S, _ = h.shape
    u = h @ w_in
    xb, g = u[..., :LRU_WIDTH], u[..., LRU_WIDTH:]
    xc = lax.conv_general_dilated(
        xb, conv_w[:, None, :].astype(xb.dtype), window_strides=(1,),
        padding=[(CONV_WIDTH - 1, 0)], dimension_numbers=('NWC', 'WIO', 'NWC'),
        feature_group_count=LRU_WIDTH) + conv_b
    xg = xc.reshape(B, S, LRU_BLOCKS, LRU_BLOCK_WIDTH)
    r = jax.nn.sigmoid(jnp.einsum('bsnc,ncd->bsnd', xg, w_a).reshape(B, S, LRU_WIDTH) + b_a)
    i = jax.nn.sigmoid(jnp.einsum('bsnc,ncd->bsnd', xg, w_x).reshape(B, S, LRU_WIDTH) + b_x)
    log_a = LRU_C * r.astype(jnp.float32) * jax.nn.log_sigmoid(lam.astype(jnp.float32))
    a = jnp.exp(log_a)
    mult = jnp.sqrt(-jnp.expm1(2.0 * log_a))
    bterm = mult * (i * xc).astype(jnp.float32)

    def combine(e1, e2):
        a1, b1 = e1
        a2, b2 = e2
        return a1 * a2, a2 * b1 + b2

    _, hs = lax.associative_scan(combine, (a, bterm), axis=1)
    y = hs.astype(h.dtype) * jax.nn.silu(g)
    return y @ w_out


def setup_inputs(seed: int = 0) -> dict:
    key = jax.random.key(seed)
    ks = jax.random.split(key, 20)
    f32 = jnp.float32
    D = D_MODEL
    x = jax.random.normal(ks[0], (BATCH, SEQ, D), f32)
    c = jax.random.normal(ks[1], (BATCH, D), f32)
    positions = jnp.broadcast_to(jnp.arange(SEQ, dtype=jnp.int32)[None, :], (BATCH, SEQ))
    norm_g = 1.0 + 0.02 * jax.random.normal(ks[2], (DEPTH, D), f32)
    w_mod = 0.5 * D ** -0.5 * jax.random.normal(ks[3], (DEPTH, D, 3 * D), f32)
    b_mod = 0.02 * jax.random.normal(ks[4], (DEPTH, 3 * D), f32)
    attn_w_in = D ** -0.5 * jax.random.normal(ks[5], (N_ATT_LAYERS, D, ATT_IN_WIDTH), f32)
    attn_w_out = ATT_OUT_WIDTH ** -0.5 * jax.random.normal(ks[6], (N_ATT_LAYERS, ATT_OUT_WIDTH, D), f32)
    lru_w_in = D ** -0.5 * jax.random.normal(ks[7], (N_LRU_LAYERS, D, 2 * LRU_WIDTH), f32)
    lru_conv_w = CONV_WIDTH ** -0.5 * jax.random.normal(ks[8], (N_LRU_LAYERS, CONV_WIDTH, LRU_WIDTH), f32)
    lru_conv_b = 0.02 * jax.random.normal(ks[9], (N_LRU_LAYERS, LRU_WIDTH), f32)
    lru_w_a = LRU_BLOCK_WIDTH ** -0.5 * jax.random.normal(
        ks[10], (N_LRU_LAYERS, LRU_BLOCKS, LRU_BLOCK_WIDTH, LRU_BLOCK_WIDTH), f32)
    lru_b_a = 0.02 * jax.random.normal(ks[11], (N_LRU_LAYERS, LRU_WIDTH), f32)
    lru_w_x = LRU_BLOCK_WIDTH ** -0.5 * jax.random.normal(
        ks[12], (N_LRU_LAYERS, LRU_BLOCKS, LRU_BLOCK_WIDTH, LRU_BLOCK_WIDTH), f32)
    lru_b_x = 0.02 * jax.random.normal(ks[13], (N_LRU_LAYERS, LRU_WIDTH), f32)
    a0 = jax.random.uniform(ks[14], (N_LRU_LAYERS, LRU_WIDTH), f32, minval=0.9, maxval=0.999)
    p = a0 ** (1.0 / LRU_C)
    lru_lambda = jnp.log(p) - jnp.log1p(-p)
    lru_w_out = LRU_WIDTH ** -0.5 * jax.random.normal(ks[15], (N_LRU_LAYERS, LRU_WIDTH, D), f32)
    final_g = 1.0 + 0.02 * jax.random.normal(ks[16], (D,), f32)
    return {"x": x, "c": c, "positions": positions, "norm_g": norm_g,
            "w_mod": w_mod, "b_mod": b_mod,
            "attn_w_in": attn_w_in, "attn_w_out": attn_w_out,
            "lru_w_in": lru_w_in, "lru_conv_w": lru_conv_w, "lru_conv_b": lru_conv_b,
            "lru_w_a": lru_w_a, "lru_b_a": lru_b_a, "lru_w_x": lru_w_x, "lru_b_x": lru_b_x,
            "lru_lambda": lru_lambda, "lru_w_out": lru_w_out, "final_g": final_g}


def reference(x, c, positions, norm_g, w_mod, b_mod, attn_w_in, attn_w_out,
              lru_w_in, lru_conv_w, lru_conv_b, lru_w_a, lru_b_a, lru_w_x, lru_b_x,
              lru_lambda, lru_w_out, final_g):
    for l in range(DEPTH):
        mod = c @ w_mod[l] + b_mod[l]
        shift, scale, gate = jnp.split(mod[:, None, :], 3, axis=-1)
        h = (rmsnorm(x, norm_g[l]) * (1.0 + scale.astype(jnp.float32))
             + shift.astype(jnp.float32)).astype(x.dtype)
        j = l // 2
        if l % 2 == 0:
            y = attention_layer(h, positions, attn_w_in[j], attn_w_out[j])
        else:
            y = rglru_layer(h, lru_w_in[j], lru_conv_w[j], lru_conv_b[j], lru_w_a[j],
                            lru_b_a[j], lru_w_x[j], lru_b_x[j], lru_lambda[j], lru_w_out[j])
        x = x + gate * y
    return rmsnorm(x, final_g).astype(x.dtype)
```

```python
import contextlib
import math
import numpy as np
import concourse.bass as bass
import concourse.mybir as mybir
from concourse.bass_utils import run_bass_kernel_spmd

F32 = mybir.dt.float32
BF16 = mybir.dt.bfloat16
I32 = mybir.dt.int32
U8 = mybir.dt.uint8
AF = mybir.ActivationFunctionType
ALU = mybir.AluOpType
AX = mybir.AxisListType

SEM_ROLL = 20000
N_DMA_SEMS = 24
D = 1024
NEG = -30000.0
EPS = 1e-6
TWO_PI = 2.0 * math.pi


class _Rec:
    def __init__(self):
        self.calls = []

    def __getattr__(self, name):
        def f(*a, **k):
            self.calls.append((name, a, k))
            return None
        return f


def _freeze(fns):
    out = []
    for fn in fns:
        r = _Rec()
        fn(r)
        assert len(r.calls) == 1, r.calls
        name, a, k = r.calls[0]
        out.append(lambda e, name=name, a=a, k=k: getattr(e, name)(*a, **k))
    return out


class Sched:
    ENG = ("pe", "act", "dve", "pool", "sp")

    def __init__(self, nc, stack):
        self.nc = nc
        self.stack = stack
        self.streams = {e: [] for e in self.ENG}
        self.sem = {}
        self.cnt = {}
        self.pe_sems = set()
        self.nsem = 0
        self.all_sems = []
        for e in self.ENG:
            self._new_eng_sem(e)
        self.dma_sems = [self._alloc_sem("dq%d" % i) for i in range(N_DMA_SEMS)]
        self.dma_val = [0] * N_DMA_SEMS
        self.dma_rr = 0
        self.sw_sems = []
        self.waited = {e: {} for e in self.ENG}
        self.last_w = {}
        self.readers = {}
        self.n_inst = 0

    def _alloc_sem(self, name):
        self.nsem += 1
        s = self.stack.enter_context(self.nc.semaphore("%s_%d" % (name, self.nsem)))
        return s

    def _new_eng_sem(self, e):
        self.sem[e] = self._alloc_sem("e_" + e)
        self.cnt[e] = 0
        self.all_sems.append([self.sem[e], 0])
        if e == "pe":
            self.pe_sems.add(id(self.sem[e]))

    def _need_waits(self, eng, events):
        best = {}
        for ev in events:
            if ev is None:
                continue
            s, v = ev
            k = id(s)
            if eng == "pe" and k in self.pe_sems:
                continue
            if self.waited[eng].get(k, 0) >= v:
                continue
            if k not in best or best[k][1] < v:
                best[k] = (s, v)
        out = []
        for k, (s, v) in best.items():
            self.waited[eng][k] = v
            out.append((s, v))
        return out

    EXCL = frozenset(["Z0", "Z1", "A0", "A1", "A2", "A3", "PJ", "psT"])

    def _deps(self, reads, writes):
        evs = []
        for k in reads:
            evs.append(self.last_w.get(k))
            if k in self.EXCL:
                evs.extend(self.readers.get(k, {}).values())
        for k in writes:
            evs.append(self.last_w.get(k))
            evs.extend(self.readers.get(k, {}).values())
        return evs

    def _commit(self, ev, reads, writes):
        for k in reads:
            d = self.readers.setdefault(k, {})
            d[id(ev[0])] = ev
        for k in writes:
            self.last_w[k] = ev
            self.readers[k] = {}

    def begin_defer(self):
        self._defer = []

    def end_defer(self):
        d, self._defer = self._defer, None
        return d

    def run_deferred(self, rec):
        kind = rec[0]
        if kind == "op":
            self.op(rec[1], rec[2], rec[3], rec[4], frozen=True)
        else:
            self.dma(rec[1], rec[2], rec[3], reads=rec[4], writes=rec[5], **rec[6])

    def op(self, eng, fns, reads=(), writes=(), extra=(), frozen=False):
        if callable(fns):
            fns = [fns]
        if not frozen:
            fns = _freeze(fns)
        if getattr(self, "_defer", None) is not None:
            self._defer.append(("op", eng, fns, tuple(reads), tuple(writes)))
            return None
        if self.cnt[eng] >= SEM_ROLL:
            self._new_eng_sem(eng)
        waits = self._need_waits(eng, self._deps(reads, writes) + list(extra))
        self.cnt[eng] += 1
        ev = (self.sem[eng], self.cnt[eng])
        self.streams[eng].append((waits, fns, ev, 1))
        self._commit(ev, reads, writes)
        self.n_inst += len(fns)
        return ev

    def dma(self, eng, out, in_, reads=(), writes=(), extra=(), **kw):
        if getattr(self, "_defer", None) is not None:
            self._defer.append(("dma", eng, out, in_, tuple(reads), tuple(writes), kw))
            return None
        if eng == "pool":
            s = self._alloc_sem("sw")
            self.sw_sems.append(s)
            waits = self._need_waits(eng, self._deps(reads, writes) + list(extra))
            ev = (s, 16)
        else:
            i = self.dma_rr
            self.dma_rr = (self.dma_rr + 1) % N_DMA_SEMS
            s = self.dma_sems[i]
            prev = (s, self.dma_val[i]) if self.dma_val[i] else None
            waits = self._need_waits(eng, self._deps(reads, writes) + list(extra) + [prev])
            self.dma_val[i] += 16
            ev = (s, self.dma_val[i])
        fn = lambda e, out=out, in_=in_, kw=kw: e.dma_start(out=out, in_=in_, **kw)
        self.streams[eng].append((waits, [fn], ev, 16))
        self._commit(ev, reads, writes)
        self.n_inst += 1
        return ev

    def all_events(self):
        evs = [(s, v) for s, v in zip(self.dma_sems, self.dma_val) if v]
        evs += [(s, 16) for s in self.sw_sems]
        for e in self.ENG:
            if self.cnt[e]:
                evs.append((self.sem[e], self.cnt[e]))
        return evs

    def barrier(self):
        evs = self.all_events()
        for e in self.ENG:
            waits = self._need_waits(e, evs)
            if waits:
                self.streams[e].append((waits, [], None, 0))

    def wait_all(self, eng, events):
        waits = self._need_waits(eng, events)
        if waits:
            self.streams[eng].append((waits, [], None, 0))

    def emit(self):
        nc = self.nc
        with nc.Block() as block:
            def run(eng_name):
                def body(e):
                    for waits, fns, ev, inc in self.streams[eng_name]:
                        for (s, v) in waits:
                            e.wait_ge(s, v)
                        for j, fn in enumerate(fns):
                            ins = fn(e)
                            if j == len(fns) - 1 and ev is not None:
                                ins.then_inc(ev[0], inc)
                return body
            block.tensor(run("pe"))
            block.scalar(run("act"))
            block.vector(run("dve"))
            block.gpsimd(run("pool"))
            block.sync(run("sp"))


class Arena:
    def __init__(self, nc, base, size):
        self.nc, self.base, self.size, self.off, self.n = nc, base, size, 0, 0
        self.peak = 0
        self.addr = {}

    def alloc(self, name, shape, dt):
        esz = {F32: 4, BF16: 2, I32: 4}[dt]
        nb = esz * int(np.prod(shape[1:]))
        nb = (nb + 63) // 64 * 64
        assert self.off + nb <= self.size, ("SBUF arena overflow", name, self.off, nb, self.size)
        self.n += 1
        h = self.nc.alloc_sbuf_tensor_at("%s_%d" % (name, self.n), list(shape), dt,
                                         offset=self.base + self.off)
        self.addr[id(h)] = self.base + self.off
        self.off += nb
        self.peak = max(self.peak, self.off)
        return h

    def alias(self, name, shape, dt, handle, byte_off=0):
        self.n += 1
        base = self.addr[id(handle)]
        h = self.nc.alloc_sbuf_tensor_at("%s_%d" % (name, self.n), list(shape), dt, offset=base + byte_off)
        self.addr[id(h)] = base + byte_off
        return h

    def mark(self):
        return self.off

    def release(self, m):
        self.off = m


K_IDENT, K_TRIM, K_NEGONES, K_MASKSB, K_MASKMB, K_PSWAP, K_ONES, K_ZERO = range(8)
KC_ROWSEL = 8 * 128
KC_COLS = 8 * 128 + 16 * 128


def host_consts():
    kc = np.zeros((128, KC_COLS), np.float32)
    p = np.arange(128)[:, None]
    j = np.arange(128)[None, :]
    kc[:, K_IDENT * 128:(K_IDENT + 1) * 128] = (p == j)
    kc[:, K_TRIM * 128:(K_TRIM + 1) * 128] = -1.0 * (p >= j)
    kc[:, K_NEGONES * 128:(K_NEGONES + 1) * 128] = -1.0
    kc[:, K_MASKSB * 128:(K_MASKSB + 1) * 128] = NEG * (p >= j)
    kc[:, K_MASKMB * 128:(K_MASKMB + 1) * 128] = NEG * (p > j)
    P = np.zeros((128, 128), np.float32)
    for hb in (0, 64):
        for i in range(8):
            P[hb + i + 8, hb + i] = -1.0
            P[hb + i, hb + 8 + i] = 1.0
    kc[:, K_PSWAP * 128:(K_PSWAP + 1) * 128] = P
    kc[:, K_ONES * 128:(K_ONES + 1) * 128] = 1.0
    for n in range(16):
        kc[n, KC_ROWSEL + n * 128: KC_ROWSEL + (n + 1) * 128] = 1.0
    kf = np.zeros((128, 128 + 128 + 2 + 256), np.float32)
    inv = 500000.0 ** (-np.arange(0, 16, 2, dtype=np.float32) / 16.0)
    for hb in (0, 64):
        for i in range(16):
            kf[0, hb + i] = inv[i % 8]
    kf[0, 128:256] = 1.0
    kf[0, 256] = 1.0
    kf[1, 257] = 1.0
    kf[0, 258:258 + 128] = 1.0
    kf[1, 258 + 128:258 + 256] = 1.0
    return kc, kf


def build(S_LEN=4096, NSEQ=2, dbg=False):
    NCH = S_LEN // 512
    NB128 = S_LEN // 128
    nc = bass.Bass("TRN2", target_bir_lowering=False)

    def din(name, shape, dt=F32):
        return nc.dram_tensor(name, list(shape), dt, kind="ExternalInput").ap()

    x_d = din("x", [NSEQ, S_LEN, D])
    cT_d = din("cT", [128, 8, NSEQ])
    pos_d = din("pos", [NSEQ, S_LEN], I32)
    ng_d = din("norm_g", [2, D])
    wmod_d = din("w_mod", [2, D, 3 * D])
    bmod_d = din("b_mod", [2, 3 * D])
    awin_d = din("attn_w_in", [D, 4096])
    awout_d = din("attn_w_out", [D, D])
    lwin_d = din("lru_w_in", [D, 2048])
    lwa_d = din("lru_w_a", [8, 128, 128])
    lwx_d = din("lru_w_x", [8, 128, 128])
    lwout_d = din("lru_w_out", [D, D])
    lcols_d = din("lru_cols", [128, 64])
    fg_d = din("final_g", [1, D])
    kc_d = din("kc", [128, KC_COLS])
    kf_d = din("kf", [128, 514])
    out_d = nc.dram_tensor("out", [NSEQ, S_LEN, D], F32, kind="ExternalOutput").ap()
    if dbg:
        dbg_y = nc.dram_tensor("dbg_y", [NSEQ, 128, 8, S_LEN], BF16, kind="ExternalOutput").ap()
        dbg_x1 = nc.dram_tensor("dbg_x1", [NSEQ, S_LEN, D], F32, kind="ExternalOutput").ap()

    with contextlib.ExitStack() as st:
        S = Sched(nc, st)
        fence = nc.alloc_sbuf_tensor("arena_fence", [128, 212800], U8)
        AR = Arena(nc, 16512, 212800)
        psb = [st.enter_context(nc.psum_tensor("psb%d" % i, [128, 512], F32)) for i in range(7)]
        psT = st.enter_context(nc.psum_tensor("psT", [128, 2, 512], BF16))
        Z = [psb[0], psb[1]]
        A = [psb[2], psb[3], psb[4], psb[5]]
        PJ = psb[6]

        kcb = AR.alloc("kcb", [128, KC_COLS], BF16)
        kfs = AR.alloc("kfs", [128, 514], F32)
        zrhs = AR.alloc("zrhs", [128, 512], BF16)
        Acol = [AR.alloc("Acol%d" % l, [128, 8, NSEQ], F32) for l in range(2)]
        Shcol = [AR.alloc("Shcol%d" % l, [128, 8, NSEQ], F32) for l in range(2)]
        grow = [AR.alloc("grow%d" % l, [NSEQ, D], F32) for l in range(2)]
        gfbc = AR.alloc("gfbc", [128, D], F32)
        lcols = AR.alloc("lcols", [128, 64], F32)
        lder = AR.alloc("lder", [128, 32], F32)
        yT = AR.alloc("yT", [128, 8, S_LEN], BF16)
        epsc = AR.alloc("epsc", [128, 1], F32)

        def C(kid, rows=128, c0=0, c1=128):
            return kcb[0:rows, kid * 128 + c0: kid * 128 + c1]

        IDENT = C(K_IDENT)
        TRIM = C(K_TRIM)
        NEGONES = C(K_NEGONES)
        MASKSB = C(K_MASKSB)
        MASKMB = C(K_MASKMB)
        PSWAP = C(K_PSWAP)
        ONES64 = C(K_ONES, 128, 0, 64)
        INVROW = kfs[0:1, 0:128]
        ONESROW = kfs[0:1, 128:256]
        I2 = kfs[0:NSEQ, 256:256 + NSEQ]

        def SEL2(b):
            return kfs[0:NSEQ, 258 + b * 128: 258 + (b + 1) * 128]

        for hf in range(2):
            S.dma("pool", kcb[:, hf * 1536:(hf + 1) * 1536], kc_d[:, hf * 1536:(hf + 1) * 1536], writes=["kcb"])
        S.dma("sp", kfs[:, :], kf_d, writes=["kfs"])
        S.dma("sp", lcols[:, :], lcols_d, writes=["lcols"])
        S.op("pool", lambda e: e.memset(zrhs[:, :], 0.0), writes=["zrhs"])
        S.op("pool", lambda e: e.memset(epsc[:, :], EPS), writes=["epsc"])

        m0 = AR.mark()
        cT = AR.alloc("cT", [128, 8, NSEQ], F32)
        modrow = [AR.alloc("modrow%d" % l, [NSEQ, 3 * D], F32) for l in range(2)]
        bmrow = [AR.alloc("bmrow%d" % l, [NSEQ, 3 * D], F32) for l in range(2)]
        grows = [AR.alloc("grows%d" % l, [NSEQ, D], F32) for l in range(2)]
        arow = [AR.alloc("arow%d" % l, [NSEQ, D], F32) for l in range(2)]
        fgrow = AR.alloc("fgrow", [1, D], F32)
        wstage = [AR.alloc("wstage%d" % i, [128, 8, 512], F32) for i in range(2)]
        S.dma("sp", cT[:, :, :], cT_d, writes=["cT"])
        S.dma("sp", fgrow[:, :], fg_d, writes=["fgrow"])
        for l in range(2):
            for b in range(NSEQ):
                S.dma("sp", bmrow[l][b:b + 1, :], bmod_d[l:l + 1, :], writes=["bmrow%d" % l])
                S.dma("sp", grows[l][b:b + 1, :], ng_d[l:l + 1, :], writes=["grows%d" % l])
        wm_v = [wmod_d[l].rearrange("(k p) f -> p k f", p=128) for l in range(2)]
        it = 0
        for l in range(2):
            for fg in range(6):
                ws = wstage[it % 2]
                wk = "wstage%d" % (it % 2)
                it += 1
                S.dma("sp", ws[:, :, :], wm_v[l][:, :, fg * 512:(fg + 1) * 512], writes=[wk])
                fns = []
                for k in range(8):
                    fns.append(lambda e, ws=ws, k=k: e.matmul(
                        PJ[0:NSEQ, :], lhsT=cT[:, k, :], rhs=ws[:, k, :], start=(k == 0), stop=(k == 7)))
                S.op("pe", fns, reads=["cT", wk], writes=["PJ"])
                S.op("dve", lambda e, l=l, fg=fg: e.tensor_tensor(
                    out=modrow[l][:, fg * 512:(fg + 1) * 512], in0=PJ[0:NSEQ, :],
                    in1=bmrow[l][:, fg * 512:(fg + 1) * 512], op=ALU.add),
                    reads=["PJ", "bmrow%d" % l], writes=["modrow%d" % l])
        for l in range(2):
            S.op("dve", lambda e, l=l: e.scalar_tensor_tensor(
                out=arow[l][:, :], in0=modrow[l][:, D:2 * D], scalar=1.0, in1=grows[l][:, :],
                op0=ALU.add, op1=ALU.mult), reads=["modrow%d" % l, "grows%d" % l], writes=["arow%d" % l])
            S.op("dve", lambda e, l=l: e.tensor_copy(out=grow[l][:, :], in_=modrow[l][:, 2 * D:3 * D]),
                 reads=["modrow%d" % l], writes=["grow%d" % l])
            fns = []
            for dc in range(8):
                fns.append(lambda e, l=l, dc=dc: e.matmul(
                    PJ[:, dc * NSEQ:(dc + 1) * NSEQ], lhsT=arow[l][:, dc * 128:(dc + 1) * 128],
                    rhs=I2, start=True, stop=True))
                fns.append(lambda e, l=l, dc=dc: e.matmul(
                    PJ[:, 64 + dc * NSEQ:64 + (dc + 1) * NSEQ], lhsT=modrow[l][:, dc * 128:(dc + 1) * 128],
                    rhs=I2, start=True, stop=True))
            S.op("pe", fns, reads=["arow%d" % l, "modrow%d" % l, "kfs"], writes=["PJ"])
            S.op("dve", lambda e, l=l: e.tensor_copy(
                out=Acol[l][:, :, :], in_=PJ[:, 0:8 * NSEQ].rearrange("p (a b) -> p a b", b=NSEQ)),
                reads=["PJ"], writes=["Acol%d" % l])
            S.op("dve", lambda e, l=l: e.tensor_copy(
                out=Shcol[l][:, :, :], in_=PJ[:, 64:64 + 8 * NSEQ].rearrange("p (a b) -> p a b", b=NSEQ)),
                reads=["PJ"], writes=["Shcol%d" % l])
        for hf in range(2):
            S.op("pe", lambda e, hf=hf: e.matmul(PJ[:, :], lhsT=ONESROW, rhs=fgrow[0:1, hf * 512:(hf + 1) * 512],
                                                 start=True, stop=True), reads=["kfs", "fgrow"], writes=["PJ"])
            S.op("dve", lambda e, hf=hf: e.tensor_copy(out=gfbc[:, hf * 512:(hf + 1) * 512], in_=PJ[:, :]),
                 reads=["PJ"], writes=["gfbc"])
        S.op("act", lambda e: e.activation(out=lder[:, 0:8], in_=lcols[:, 56:64], func=AF.Exp, scale=-1.0),
             reads=["lcols"], writes=["lder"])
        S.op("act", lambda e: e.activation(out=lder[:, 8:16], in_=lder[:, 0:8], func=AF.Ln, bias=1.0),
             reads=["lder"], writes=["lder"])
        S.op("dve", lambda e: e.tensor_scalar(out=lder[:, 0:8], in0=lder[:, 8:16], scalar1=-8.0, scalar2=None,
                                              op0=ALU.mult), reads=["lder"], writes=["lder"])
        S.op("dve", lambda e: e.tensor_scalar(out=lder[:, 8:16], in0=lder[:, 0:8], scalar1=2.0, scalar2=None,
                                              op0=ALU.mult), reads=["lder"], writes=["lder"])
        S.op("dve", lambda e: e.tensor_scalar(out=lder[:, 16:32], in0=lcols[:, 40:56], scalar1=-1.0, scalar2=None,
                                              op0=ALU.mult), reads=["lcols", "lder"], writes=["lder"])
        S.barrier()
        AR.release(m0)

        xv = x_d

        for b in range(NSEQ):
            mA = AR.mark()
            nrm = {"ssq": AR.alloc("ssq", [128, 4], F32), "lnv": AR.alloc("lnv", [128, 4], F32),
                   "rstd": AR.alloc("rstd", [128, 4], F32)}
            Wg = AR.alloc("Wg", [128, 8, 4, 256], BF16)
            kTc = AR.alloc("kTc", [128, 2, S_LEN], BF16)
            Vc = AR.alloc("Vc", [128, NB128, 256], BF16)
            hTc = AR.alloc("hT", [128, 8, 512], BF16)
            xt = [AR.alloc("xt%d" % i, [128, D], F32) for i in range(2)]
            xs = AR.alloc("xs", [128, 4, D], BF16)
            qT = [AR.alloc("qT%d" % i, [128, 2, 2, 512], BF16) for i in range(2)]
            sg = [AR.alloc("sg%d" % i, [128, 2, 512], BF16) for i in range(2)]
            negmT = [AR.alloc("negmT%d" % i, [128, 4, 512], BF16) for i in range(2)]
            Eb = [AR.alloc("Eb%d" % i, [128, 512], F32) for i in range(2)]
            Lb = [AR.alloc("Lb%d" % i, [128, 512], BF16) for i in range(2)]
            Ls = [AR.alloc("Ls%d" % i, [128, 512], BF16) for i in range(2)]
            Wt = [AR.alloc("Wt%d" % i, [128, 512], BF16) for i in range(2)]
            rtmp = AR.alloc("rtmp", [128, 512], BF16)
            t32 = [AR.alloc("t32_%d" % i, [128, 512], F32) for i in range(4)]
            ki = AR.alias("ki", [128, 512], I32, t32[3])
            posi = AR.alias("posi", [1, 512], I32, t32[2])
            posf = AR.alias("posf", [1, 512], F32, t32[1])
            q32 = AR.alloc("q32", [128, 2, 512], F32)
            kmean = AR.alloc("kmean", [128, 2, 16], F32)
            cs = [AR.alias("cossin0", [128, 512], F32, Lb[0]), AR.alias("cossin1", [128, 512], F32, Ls[0])]
            gp = AR.alias("gp", [128, 16, 16], F32, Eb[1])
            m8 = AR.alias("m8", [128, 16, 8], F32, Eb[1], 1024)
            negm = AR.alias("negm", [128, 16, 16], BF16, Eb[1], 1536)
            rden = AR.alias("rden", [128, 512], F32, Eb[0])
            xload_i = [0]
            for par in range(2):
                S.op("pool", lambda e, par=par: e.memset(qT[par][:, :, :, :], 0.0),
                     writes=["qT%d_%d" % (fc, par) for fc in range(2)])
                S.op("pool", lambda e, par=par: e.memset(negmT[par][:, :, :], 0.0),
                     writes=["negmT%d_%d" % (h, par) for h in range(4)])
            _kinds = ("sb", "mb")
            def load_Wg(kind, g):
                if kind == "sb":
                    cols = [0 + g * 256, 512 + g * 256, 1024 + g * 256, 3072 + g * 256]
                else:
                    cols = [1536 + g * 256, 2048 + g * 256, 2560 + g * 256, 3584 + g * 256]
                wv = awin_d.rearrange("(k p) c -> p k c", p=128)
                for j in range(4):
                    S.dma("pool", Wg[:, :, j, :], wv[:, :, cols[j]:cols[j] + 256], writes=["Wg"])

            passes = [(kind, g) for kind in _kinds for g in range(2)]
            wg_loaded = [False]
            for pi, (kind, g) in enumerate(passes):
                if True:
                    S.barrier()
                    if not wg_loaded[0]:
                        load_Wg(kind, g)
                    wg_loaded[0] = False
                    if kind == "mb":
                        S.op("pool", lambda e: e.memset(gp[:, :, :], -1e30), writes=["gp"])
                        S.op("pool", lambda e: e.memset(kmean[:, :, :], 0.0), writes=["kmean0", "kmean1"])
                    ych0 = (0 if kind == "sb" else 4) + 2 * g

                    def prep(c):
                        par = c % 2
                        qTp, sgp, nmp = qT[par], sg[par], negmT[par]
                        hkeys = ["hT%d" % dc for dc in range(8)]
                        if kind == "mb":
                            S.dma("sp", posi[:, :], pos_d[b:b + 1, c * 512:(c + 1) * 512], writes=["t32_2"])
                            S.op("dve", lambda e: e.tensor_copy(out=posf[:, :], in_=posi[:, :]),
                                 reads=["t32_2"], writes=["t32_1"])
                            S.op("pe", lambda e: e.matmul(PJ[:, :], lhsT=INVROW, rhs=posf[0:1, :],
                                                          start=True, stop=True),
                                 reads=["kfs", "t32_1"], writes=["PJ"])
                            S.op("dve", lambda e: e.tensor_scalar(out=ki[:, :], in0=PJ[:, :], scalar1=1.0 / TWO_PI,
                                                                  scalar2=None, op0=ALU.mult),
                                 reads=["PJ"], writes=["t32_3"])
                            S.op("dve", lambda e: e.tensor_copy(out=t32[0][:, :], in_=ki[:, :]),
                                 reads=["t32_3"], writes=["t32_0"])
                            S.op("dve", lambda e: e.scalar_tensor_tensor(
                                out=t32[1][:, :], in0=t32[0][:, :], scalar=-TWO_PI, in1=PJ[:, :],
                                op0=ALU.mult, op1=ALU.add), reads=["t32_0", "PJ"], writes=["t32_1"])
                            S.op("dve", lambda e: e.tensor_scalar(
                                out=t32[2][:, :], in0=t32[1][:, :], scalar1=math.pi, scalar2=-TWO_PI,
                                op0=ALU.is_gt, op1=ALU.mult), reads=["t32_1"], writes=["t32_2"])
                            S.op("dve", lambda e: e.tensor_tensor(
                                out=t32[2][:, :], in0=t32[2][:, :], in1=t32[1][:, :], op=ALU.add),
                                reads=["t32_1", "t32_2"], writes=["t32_2"])
                            S.op("dve", lambda e: e.tensor_scalar(
                                out=t32[1][:, :], in0=t32[1][:, :], scalar1=math.pi / 2, scalar2=None,
                                op0=ALU.add), reads=["t32_1"], writes=["t32_1"])
                            S.op("dve", lambda e: e.tensor_scalar(
                                out=t32[0][:, :], in0=t32[1][:, :], scalar1=math.pi, scalar2=-TWO_PI,
                                op0=ALU.is_gt, op1=ALU.mult), reads=["t32_1"], writes=["t32_0"])
                            S.op("dve", lambda e: e.tensor_tensor(
                                out=t32[0][:, :], in0=t32[0][:, :], in1=t32[1][:, :], op=ALU.add),
                                reads=["t32_1", "t32_0"], writes=["t32_0"])
                            S.op("act", [lambda e: e.activation(out=cs[0][:, :], in_=t32[2][:, :], func=AF.Sin),
                                         lambda e: e.activation(out=cs[1][:, :], in_=t32[0][:, :], func=AF.Sin)],
                                 reads=["t32_2", "t32_0"], writes=["cs0", "cs1"])

                        ssq = nrm["ssq"]
                        slots = {}

                        def xload(tt):
                            i = xload_i[0] % 2
                            xload_i[0] += 1
                            t0 = c * 512 + tt * 128
                            S.dma("sp", xt[i][:, :], xv[b, t0:t0 + 128, :], writes=["xt%d" % i])
                            slots[tt] = i
                        xload(0)
                        xload(1)
                        for tt in range(4):
                            i = slots[tt]
                            xk = "xt%d" % i
                            xa = xt[i][:, :]
                            S.op("dve", lambda e: e.scalar_tensor_tensor(
                                out=xs[:, tt, :], in0=xa, scalar=1.0, in1=xa, op0=ALU.mult, op1=ALU.mult,
                                accum_out=ssq[:, tt:tt + 1]),
                                reads=[xk], writes=["xs%d" % tt, "ssq%d" % tt])
                            S.op("act", lambda e: e.activation(
                                out=nrm["lnv"][:, tt:tt + 1], in_=ssq[:, tt:tt + 1], func=AF.Ln,
                                scale=1.0 / D, bias=epsc[:, 0:1]),
                                reads=["ssq%d" % tt, "epsc"], writes=["lnv%d" % tt])
                            S.op("act", lambda e: e.activation(
                                out=nrm["rstd"][:, tt:tt + 1], in_=nrm["lnv"][:, tt:tt + 1], func=AF.Exp,
                                scale=-0.5), reads=["lnv%d" % tt], writes=["rstd%d" % tt])
                            S.op("pool", lambda e: e.tensor_scalar(
                                out=xs[:, tt, :], in0=xa, scalar1=nrm["rstd"][:, tt:tt + 1], scalar2=0.0,
                                op0=ALU.mult, op1=ALU.add), reads=[xk, "rstd%d" % tt], writes=["xs%d" % tt])
                            if tt + 2 < 4:
                                xload(tt + 2)
                        for dr in range(4):
                            S.op("pe", [lambda e, tt=tt, dc=dc: e.transpose(
                                out=psT[:, dc % 2, tt * 128:(tt + 1) * 128],
                                in_=xs[:, tt, dc * 128:(dc + 1) * 128], identity=IDENT)
                                for dc in (2 * dr, 2 * dr + 1) for tt in range(4)],
                                reads=["xs%d" % t for t in range(4)] + ["kcb"], writes=["psT"])
                            S.op("dve", [lambda e, dc=dc: e.tensor_scalar(
                                out=hTc[:, dc, :], in0=psT[:, dc % 2, :],
                                scalar1=Acol[0][:, dc, b:b + 1], scalar2=Shcol[0][:, dc, b:b + 1],
                                op0=ALU.mult, op1=ALU.add) for dc in (2 * dr, 2 * dr + 1)],
                                reads=["psT", "Acol0", "Shcol0"], writes=["hT%d" % (2 * dr), "hT%d" % (2 * dr + 1)])

                        def proj_fm(j, fc):
                            S.op("pe", [lambda e, dc=dc: e.matmul(
                                PJ[:, :], lhsT=Wg[:, dc, j, fc * 128:(fc + 1) * 128], rhs=hTc[:, dc, :],
                                start=(dc == 0), stop=(dc == 7)) for dc in range(8)],
                                reads=hkeys + ["Wg"], writes=["PJ"])

                        def rope(dst32, dkey, scale):
                            S.op("act", lambda e: e.activation(out=rtmp[:, :], in_=PJ[:, :], func=AF.Copy,
                                                               scale=scale), reads=["PJ"], writes=["rtmp"])
                            S.op("dve", lambda e: e.scalar_tensor_tensor(
                                out=t32[0][:, :], in0=PJ[:, :], scalar=scale, in1=cs[1][:, :],
                                op0=ALU.mult, op1=ALU.mult), reads=["PJ", "cs1", "rtmp"], writes=["t32_0"])
                            S.op("pe", lambda e: e.matmul(PJ[:, :], lhsT=PSWAP, rhs=rtmp[:, :],
                                                          start=True, stop=True),
                                 reads=["rtmp", "kcb", "t32_0"], writes=["PJ"])
                            S.op("dve", lambda e: e.tensor_tensor(out=t32[1][:, :], in0=PJ[:, :],
                                                                  in1=cs[0][:, :], op=ALU.mult),
                                 reads=["PJ", "cs0"], writes=["t32_1"])
                            S.op("dve", lambda e: e.tensor_tensor(
                                out=dst32, in0=t32[0][:, :], in1=t32[1][:, :], op=ALU.add),
                                reads=["t32_0", "t32_1"], writes=[dkey])

                        for fc in range(2):
                            qk = "qT%d_%d" % (fc, par)
                            proj_fm(0, fc)
                            if kind == "sb":
                                S.op("dve", [lambda e, hh=hh: e.tensor_scalar(
                                    out=qTp[hh * 64:(hh + 1) * 64, fc, hh, :], in0=PJ[hh * 64:(hh + 1) * 64, :],
                                    scalar1=0.125, scalar2=None, op0=ALU.mult) for hh in range(2)],
                                    reads=["PJ"], writes=[qk])
                            else:
                                rope(q32[:, fc, :], "q32_%d" % fc, 0.125)
                                S.op("act", [lambda e, hh=hh: e.activation(
                                    out=qTp[hh * 64:(hh + 1) * 64, fc, hh, :], in_=q32[hh * 64:(hh + 1) * 64, fc, :],
                                    func=AF.Copy) for hh in range(2)], reads=["q32_%d" % fc], writes=[qk])
                            proj_fm(1, fc)
                            kkey = "kT%d_%d" % (fc, c)
                            if kind == "sb":
                                S.op("dve", lambda e: e.tensor_copy(
                                    out=kTc[:, fc, c * 512:(c + 1) * 512], in_=PJ[:, :]),
                                    reads=["PJ"], writes=[kkey])
                            else:
                                rope(t32[3][:, :], "t32_3", 1.0)
                                S.op("act", lambda e: e.activation(
                                    out=kTc[:, fc, c * 512:(c + 1) * 512], in_=t32[3][:, :], func=AF.Copy),
                                    reads=["t32_3"], writes=[kkey])
                                S.op("dve", lambda e: e.tensor_reduce(
                                    out=kmean[:, fc, 2 * c:2 * c + 2],
                                    in_=t32[3][:, :].rearrange("p (a b) -> p a b", a=2),
                                    axis=AX.X, op=ALU.add), reads=["t32_3"], writes=["kmean%d" % fc])
                                S.op("dve", lambda e: e.tensor_scalar(
                                    out=kmean[:, fc, 2 * c:2 * c + 2], in0=kmean[:, fc, 2 * c:2 * c + 2],
                                    scalar1=1.0 / 256.0, scalar2=None, op0=ALU.mult),
                                    reads=["kmean%d" % fc], writes=["kmean%d" % fc])
                            proj_fm(3, fc)
                            S.op("act", lambda e: e.activation(out=t32[0][:, :], in_=PJ[:, :], func=AF.Exp,
                                                               scale=-1.0), reads=["PJ"], writes=["t32_0"])
                            S.op("act", lambda e: e.activation(out=t32[1][:, :], in_=t32[0][:, :], func=AF.Ln,
                                                               bias=1.0), reads=["t32_0"], writes=["t32_1"])
                            S.op("act", lambda e: e.activation(out=t32[0][:, :], in_=t32[1][:, :], func=AF.Exp,
                                                               scale=-1.0), reads=["t32_1"], writes=["t32_0"])
                            S.op("dve", lambda e: e.tensor_tensor(
                                out=sgp[:, fc, :], in0=PJ[:, :], in1=t32[0][:, :], op=ALU.mult),
                                reads=["PJ", "t32_0"], writes=["sg%d_%d" % (fc, par)])
                        for tt in range(4):
                            S.op("pe", [lambda e, dc=dc: e.matmul(
                                PJ[:, 0:256], lhsT=hTc[:, dc, tt * 128:(tt + 1) * 128], rhs=Wg[:, dc, 2, :],
                                start=(dc == 0), stop=(dc == 7)) for dc in range(8)],
                                reads=hkeys + ["Wg"], writes=["PJ"])
                            S.op("dve", lambda e: e.tensor_copy(out=Vc[:, c * 4 + tt, :], in_=PJ[:, 0:256]),
                                 reads=["PJ"], writes=["V%d" % c])
                        if kind == "mb":
                            tts = [tt for tt in range(4) if 2 * c + (tt >= 2) > 0]
                            sls = [h * 4 + tt for h in range(4) for tt in tts]
                            S.op("pe", [lambda e, h=h, tt=tt: e.matmul(
                                PJ[:, (h * 4 + tt) * 16:(h * 4 + tt + 1) * 16],
                                lhsT=q32[(h % 2) * 64:(h % 2) * 64 + 64, h // 2, tt * 128:(tt + 1) * 128],
                                rhs=kmean[(h % 2) * 64:(h % 2) * 64 + 64, h // 2, 0:16], start=True, stop=True)
                                for h in range(4) for tt in tts],
                                reads=["q32_0", "q32_1", "kmean0", "kmean1"], writes=["PJ"])
                            pjv = PJ[:, 0:256].rearrange("p (h t n) -> p h t n", h=4, t=4)
                            gpv = gp[:, :, :].rearrange("p (h t) n -> p h t n", h=4)
                            fns = []
                            nv0, nv1 = min(2 * c, 16), min(2 * c + 1, 16)
                            if nv0 > 0:
                                fns.append(lambda e: e.tensor_copy(out=gpv[:, :, 0:2, 0:nv0], in_=pjv[:, :, 0:2, 0:nv0]))
                            fns.append(lambda e: e.tensor_copy(out=gpv[:, :, 2:4, 0:nv1], in_=pjv[:, :, 2:4, 0:nv1]))
                            S.op("dve", fns, reads=["PJ", "gp"], writes=["gpall"])
                            S.op("dve", [lambda e, sl=sl: e.max(out=m8[:, sl, :], in_=gp[:, sl, :]) for sl in sls],
                                 reads=["gp", "gpall"], writes=["m8all"])
                            S.op("dve", [lambda e, sl=sl: e.tensor_scalar(
                                out=negm[:, sl, :], in0=gp[:, sl, :], scalar1=m8[:, sl, 2:3], scalar2=NEG,
                                op0=ALU.is_lt, op1=ALU.mult) for sl in sls],
                                reads=["gp", "gpall", "m8all"], writes=["negmall"])
                            c0 = tts[0] * 128
                            for hp in range(2):
                                S.op("pe", [lambda e, hh=hh, tt=tt: e.transpose(
                                    out=psT[0:16, hh, tt * 128:(tt + 1) * 128], in_=negm[:, (2 * hp + hh) * 4 + tt, :],
                                    identity=IDENT) for hh in range(2) for tt in tts],
                                    reads=["negmall", "kcb"], writes=["psT"])
                                S.op("dve", lambda e: e.tensor_copy(
                                    out=nmp[0:16, 2 * hp:2 * hp + 2, c0:512], in_=psT[0:16, 0:2, c0:512]),
                                    reads=["psT"], writes=["negmT%d_%d" % (2 * hp, par), "negmT%d_%d" % (2 * hp + 1, par)])

                    def sb_tiles(c, pump):
                        par = c % 2
                        qTp, sgp = qT[par], sg[par]
                        tiles = []
                        for h in range(4):
                            kbs = list(range(4 * c + 3, -1, -1))
                            for i, kb in enumerate(kbs):
                                j = kb - 4 * c
                                lo = 128 * j if j >= 0 else 0
                                tiles.append(dict(h=h, kb=kb, lo=lo, diag=(j >= 0), first=(i == 0),
                                                  last=(kb == 0), idx=len(tiles)))
                        for i, T in enumerate(tiles):
                            T["lo_prev"] = None if T["first"] else tiles[i - 1]["lo"]

                        def qk_fn(T, dst, start, stop):
                            h, kb, lo = T["h"], T["kb"], T["lo"]
                            fc = h // 2
                            return lambda e: e.matmul(dst[:, lo:512], lhsT=kTc[:, fc, kb * 128:(kb + 1) * 128],
                                                      rhs=qTp[:, fc, h % 2, lo:512], start=start, stop=stop)

                        def emit_Z(T):
                            zb = T["idx"] % 2
                            h, kb, lo = T["h"], T["kb"], T["lo"]
                            fc = h // 2
                            fns = [qk_fn(T, Z[zb], True, not T["diag"])]
                            if T["diag"]:
                                fns.append(lambda e: e.matmul(Z[zb][:, lo:lo + 128], lhsT=IDENT, rhs=MASKSB,
                                                              start=False, stop=True))
                            S.op("pe", fns, reads=["kT%d_%d" % (fc, kb // 4), "qT%d_%d" % (fc, par), "kcb"],
                                 writes=["Z%d" % zb])

                        def emit_E(T):
                            zb = T["idx"] % 2
                            lo = T["lo"]
                            S.op("act", lambda e: e.activation(out=Z[zb][:, lo:512], in_=Z[zb][:, lo:512],
                                                               func=AF.Exp),
                                 reads=["Z%d" % zb], writes=["Z%d" % zb])

                        def emit_L(T):
                            zb = T["idx"] % 2
                            lo = T["lo"]
                            S.op("act", lambda e: e.activation(out=Lb[zb][:, lo:512], in_=Z[zb][:, lo:512],
                                                               func=AF.Ln, bias=1.0),
                                 reads=["Z%d" % zb], writes=["Lb%d" % zb])

                        def emit_B(T):
                            zb = T["idx"] % 2
                            h, kb, lo = T["h"], T["kb"], T["lo"]
                            fc = h // 2
                            lso = (T["idx"] + 1) % 2
                            fns = [lambda e: e.matmul(A[zb][:, lo:512], lhsT=TRIM, rhs=Lb[zb][:, lo:512],
                                                      start=True, stop=False)]
                            rd = ["Lb%d" % zb, "kT%d_%d" % (fc, kb // 4), "qT%d_%d" % (fc, par), "kcb"]
                            if not T["first"]:
                                lp = T["lo_prev"]
                                fns.append(lambda e: e.matmul(A[zb][:, lp:512], lhsT=NEGONES,
                                                              rhs=Ls[lso][:, lp:512], start=False, stop=False))
                                rd.append("Ls%d" % lso)
                            fns.append(qk_fn(T, A[zb], False, not T["diag"]))
                            if T["diag"]:
                                fns.append(lambda e: e.matmul(A[zb][:, lo:lo + 128], lhsT=IDENT, rhs=MASKSB,
                                                              start=False, stop=True))
                            S.op("pe", fns, reads=rd, writes=["A%d" % zb])

                        def emit_LsUpd(T):
                            if T["last"]:
                                return
                            zb = T["idx"] % 2
                            lo = T["lo"]
                            lsn = T["idx"] % 2
                            lso = (T["idx"] + 1) % 2
                            if T["first"]:
                                S.op("dve", lambda e: e.tensor_copy(out=Ls[lsn][:, lo:512], in_=Lb[zb][:, lo:512]),
                                     reads=["Lb%d" % zb], writes=["Ls%d" % lsn])
                            else:
                                lp = T["lo_prev"]
                                fns = []
                                if lo < lp:
                                    fns.append(lambda e: e.tensor_copy(out=Ls[lsn][:, lo:lp], in_=Lb[zb][:, lo:lp]))
                                fns.append(lambda e: e.tensor_tensor(
                                    out=Ls[lsn][:, lp:512], in0=Ls[lso][:, lp:512], in1=Lb[zb][:, lp:512],
                                    op=ALU.add))
                                S.op("dve", fns, reads=["Lb%d" % zb, "Ls%d" % lso], writes=["Ls%d" % lsn])

                        def emit_W(T):
                            zb = T["idx"] % 2
                            lo = T["lo"]
                            S.op("act", lambda e: e.activation(out=Wt[zb][:, lo:512], in_=A[zb][:, lo:512],
                                                               func=AF.Exp),
                                 reads=["A%d" % zb], writes=["Wt%d" % zb])

                        def emit_PV(T):
                            zb = T["idx"] % 2
                            h, kb, lo = T["h"], T["kb"], T["lo"]
                            fc, hh = h // 2, h % 2
                            O = A[2 + hh]
                            ok = "A%d" % (2 + hh)
                            fns = []
                            if T["first"]:
                                fns.append(lambda e: e.matmul(O[:, :], lhsT=IDENT, rhs=zrhs[:, :],
                                                              start=True, stop=False))
                            fns.append(lambda e: e.matmul(O[:, lo:512], lhsT=Vc[:, kb, fc * 128:(fc + 1) * 128],
                                                          rhs=Wt[zb][:, lo:512], start=False, stop=T["last"]))
                            S.op("pe", fns, reads=["Wt%d" % zb, "V%d" % (kb // 4), "zrhs", "kcb"], writes=[ok])
                            if T["last"]:
                                r0, r1 = hh * 64, (hh + 1) * 64
                                S.op("dve", lambda e: e.tensor_tensor(
                                    out=yT[r0:r1, ych0 + fc, c * 512:(c + 1) * 512], in0=O[r0:r1, :],
                                    in1=sgp[r0:r1, fc, :], op=ALU.mult),
                                    reads=[ok, "sg%d_%d" % (fc, par)], writes=["yT"])

                        n = len(tiles)
                        emit_Z(tiles[0])
                        if n > 1:
                            emit_Z(tiles[1])
                        emit_E(tiles[0])
                        for i in range(n):
                            if i + 1 < n:
                                emit_E(tiles[i + 1])
                            emit_L(tiles[i])
                            if i + 2 < n:
                                emit_Z(tiles[i + 2])
                            emit_B(tiles[i])
                            emit_LsUpd(tiles[i])
                            if i >= 1:
                                emit_W(tiles[i - 1])
                                emit_PV(tiles[i - 1])
                            pump(i, n)
                        emit_W(tiles[n - 1])
                        emit_PV(tiles[n - 1])

                    def mb_tiles(c, pump):
                        par = c % 2
                        qTp, sgp, nmp = qT[par], sg[par], negmT[par]
                        tiles = []
                        for h in range(4):
                            kbs = list(range(0, 4 * c + 4))
                            for i, kb in enumerate(kbs):
                                j = kb - 4 * c
                                lo = 128 * j if j >= 0 else 0
                                tiles.append(dict(h=h, kb=kb, lo=lo, j=j, first=(i == 0),
                                                  last=(i == len(kbs) - 1), idx=len(tiles)))

                        def emit_Zm(T):
                            zb = T["idx"] % 2
                            h, kb, lo, j = T["h"], T["kb"], T["lo"], T["j"]
                            fc = h // 2
                            nblk = kb // 2
                            extra = []
                            if j < 0:
                                extra.append((0, 512, "sel"))
                            elif j in (0, 1):
                                extra.append((256, 512, "sel"))
                            if j >= 0:
                                extra.append((lo, lo + 128, "tri"))
                            fns = [lambda e: e.matmul(
                                Z[zb][:, lo:512], lhsT=kTc[:, fc, kb * 128:(kb + 1) * 128],
                                rhs=qTp[:, fc, h % 2, lo:512], start=True, stop=False)]
                            for ei, (a0, a1, kd) in enumerate(extra):
                                lastm = (ei == len(extra) - 1)
                                if kd == "sel":
                                    fns.append(lambda e, a0=a0, a1=a1, lastm=lastm: e.matmul(
                                        Z[zb][:, a0:a1],
                                        lhsT=kcb[:, KC_ROWSEL + nblk * 128: KC_ROWSEL + (nblk + 1) * 128],
                                        rhs=nmp[:, h, a0:a1], start=False, stop=lastm))
                                else:
                                    fns.append(lambda e, a0=a0, a1=a1, lastm=lastm: e.matmul(
                                        Z[zb][:, a0:a1], lhsT=IDENT, rhs=MASKMB, start=False, stop=lastm))
                            S.op("pe", fns, reads=["kT%d_%d" % (fc, kb // 4), "qT%d_%d" % (fc, par), "kcb",
                                                   "negmT%d_%d" % (h, par)], writes=["Z%d" % zb])

                        def emit_Wm(T):
                            zb = T["idx"] % 2
                            lo = T["lo"]
                            S.op("act", lambda e: e.activation(out=Wt[zb][:, lo:512], in_=Z[zb][:, lo:512],
                                                               func=AF.Exp),
                                 reads=["Z%d" % zb], writes=["Wt%d" % zb])

                        def emit_PVm(T):
                            zb = T["idx"] % 2
                            h, kb, lo = T["h"], T["kb"], T["lo"]
                            fc, hh = h // 2, h % 2
                            O, Dn = A[hh], A[2 + hh]
                            ok = ["A%d" % hh, "A%d" % (2 + hh)]
                            fns = [lambda e: e.matmul(O[:, lo:512], lhsT=Vc[:, kb, fc * 128:(fc + 1) * 128],
                                                      rhs=Wt[zb][:, lo:512], start=T["first"], stop=T["last"]),
                                   lambda e: e.matmul(Dn[:, lo:512], lhsT=C(K_ONES),
                                                      rhs=Wt[zb][:, lo:512], start=T["first"], stop=T["last"])]
                            S.op("pe", fns, reads=["Wt%d" % zb, "V%d" % (kb // 4), "kcb"], writes=ok)
                            if T["last"]:
                                r0, r1 = hh * 64, (hh + 1) * 64
                                rk = "rden%d" % hh
                                S.op("act", lambda e: e.activation(out=rden[r0:r1, :], in_=Dn[r0:r1, :],
                                                                   func=AF.Ln), reads=ok, writes=[rk])
                                S.op("act", lambda e: e.activation(out=rden[r0:r1, :], in_=rden[r0:r1, :],
                                                                   func=AF.Exp, scale=-1.0), reads=[rk], writes=[rk])
                                S.op("dve", lambda e: e.tensor_tensor(out=rden[r0:r1, :], in0=O[r0:r1, :],
                                                                      in1=rden[r0:r1, :], op=ALU.mult),
                                     reads=ok + [rk], writes=[rk])
                                S.op("dve", lambda e: e.tensor_tensor(
                                    out=yT[r0:r1, ych0 + fc, c * 512:(c + 1) * 512], in0=rden[r0:r1, :],
                                    in1=sgp[r0:r1, fc, :], op=ALU.mult),
                                    reads=[rk, "sg%d_%d" % (fc, par)], writes=["yT"])

                        n = len(tiles)
                        emit_Zm(tiles[0])
                        for i in range(n):
                            emit_Wm(tiles[i])
                            if i + 1 < n:
                                emit_Zm(tiles[i + 1])
                            emit_PVm(tiles[i])
                            pump(i, n)

                    prep(0)
                    for c in range(NCH):
                        if c + 1 < NCH:
                            S.begin_defer()
                            prep(c + 1)
                            pend = S.end_defer()
                        else:
                            pend = []
                        state = {"k": 0}

                        def pump(i, n, pend=pend, state=state):
                            span = max(1, int(n * 0.85))
                            target = len(pend) if i + 1 >= span else (len(pend) * (i + 1)) // span
                            while state["k"] < target:
                                S.run_deferred(pend[state["k"]])
                                state["k"] += 1
                        if c == NCH - 1 and pi + 1 < len(passes):
                            load_Wg(*passes[pi + 1])
                            wg_loaded[0] = True
                        if kind == "sb":
                            sb_tiles(c, pump)
                        else:
                            mb_tiles(c, pump)
                        while state["k"] < len(pend):
                            S.run_deferred(pend[state["k"]])
                            state["k"] += 1
            S.barrier()
            AR.release(mA)
            if dbg:
                S.dma("sp", dbg_y[b], yT[:, :, :], reads=["yT"])

            mB = AR.mark()
            TB = 256
            NTB = S_LEN // TB
            nrm = {"ssq": AR.alloc("ssqB", [128, 4], F32), "lnv": AR.alloc("lnvB", [128, 4], F32),
                   "rstd": AR.alloc("rstdB", [128, 4], F32)}
            wo0 = AR.alloc("wo0", [128, 8, D], BF16)
            wo1 = AR.alloc("wo1", [128, 8, D], BF16)
            wli = AR.alloc("wli", [128, 8, 2048], BF16)
            wax = AR.alloc("wax", [128, 2, 8, 128], BF16)
            x1b = [AR.alloc("x1_%d" % i, [128, 2, D], F32) for i in range(2)]
            xsB = AR.alloc("xsB", [128, 2, D], BF16)
            h1T = AR.alloc("h1T", [128, 8, TB], BF16)
            y1T = AR.alloc("y1T", [128, 8, TB], BF16)
            xbuf = AR.alloc("xbuf", [128, 8, 4], F32)
            xbw2 = [AR.alloc("xbw%d" % i, [128, TB + 3], F32) for i in range(2)]
            xc2 = [AR.alloc("xc%d" % i, [128, TB], F32) for i in range(2)]
            xcb2 = [AR.alloc("xcb%d" % i, [128, TB], BF16) for i in range(2)]
            f32t2 = [[AR.alloc("f32t%d_%d" % (p, i), [128, TB], F32) for i in range(6)] for p in range(2)]
            hst = AR.alloc("hst", [128, 8], F32)
            outt = [AR.alloc("outt%d" % i, [128, D], F32) for i in range(2)]

            S.dma("pool", wo0[:, :, :], awout_d.rearrange("(k p) c -> p k c", p=128), writes=["wo0"])
            for hf in range(2):
                S.dma("pool", wli[:, :, hf * 1024:(hf + 1) * 1024],
                      lwin_d.rearrange("(k p) c -> p k c", p=128)[:, :, hf * 1024:(hf + 1) * 1024],
                      writes=["wli"])
            S.dma("pool", wax[:, 0, :, :], lwa_d.rearrange("n c d -> c n d"), writes=["wax"])
            S.dma("pool", wax[:, 1, :, :], lwx_d.rearrange("n c d -> c n d"), writes=["wax"])
            S.dma("pool", wo1[:, :, :], lwout_d.rearrange("(k p) c -> p k c", p=128), writes=["wo1"])

            def scale_wo(l, wo):
                for hf in range(2):
                    S.op("pe", lambda e, l=l, hf=hf: e.matmul(
                        PJ[:, :], lhsT=SEL2(b), rhs=grow[l][:, hf * 512:(hf + 1) * 512], start=True, stop=True),
                        reads=["kfs", "grow%d" % l], writes=["PJ"])
                    for k in range(8):
                        S.op("dve", lambda e, wo=wo, k=k, hf=hf: e.tensor_tensor(
                            out=wo[:, k, hf * 512:(hf + 1) * 512], in0=PJ[:, :],
                            in1=wo[:, k, hf * 512:(hf + 1) * 512], op=ALU.mult),
                            reads=["PJ", "wo%d" % l], writes=["wo%d" % l])
            scale_wo(0, wo0)
            S.op("pool", lambda e: e.memset(xbuf[:, :, :], 0.0), writes=["xbuf"])
            S.op("pool", lambda e: e.memset(hst[:, :], 0.0), writes=["hst"])

            CW = lambda k, n: lcols[:, k * 8 + n: k * 8 + n + 1]
            CB = lambda n: lcols[:, 32 + n: 33 + n]
            KC1 = lambda n: lder[:, n:n + 1]
            KC2 = lambda n: lder[:, 8 + n: 9 + n]
            NBA = lambda n: lder[:, 16 + n: 17 + n]
            NBX = lambda n: lder[:, 24 + n: 25 + n]
            xli = [0]
            oti = [0]
            def stageP(cb):
                t0c = cb * TB
                q = cb % 2
                x1 = x1b[q]
                for tt in range(2):
                    S.dma("sp", x1[:, tt, :], xv[b, t0c + tt * 128: t0c + (tt + 1) * 128, :],
                          writes=["x1_%d_%d" % (q, tt)])
                for tt in range(2):
                    for hf in range(2):
                        bank = A[(tt * 2 + hf) % 4]
                        bk = "A%d" % ((tt * 2 + hf) % 4)
                        S.op("pe", [lambda e, k=k: e.matmul(
                            bank[:, :], lhsT=yT[:, k, t0c + tt * 128: t0c + (tt + 1) * 128],
                            rhs=wo0[:, k, hf * 512:(hf + 1) * 512], start=(k == 0), stop=(k == 7))
                            for k in range(8)], reads=["yT", "wo0"], writes=[bk])
                        S.op("dve", lambda e: e.tensor_tensor(
                            out=x1[:, tt, hf * 512:(hf + 1) * 512], in0=bank[:, :],
                            in1=x1[:, tt, hf * 512:(hf + 1) * 512], op=ALU.add),
                            reads=[bk, "x1_%d_%d" % (q, tt)], writes=["x1_%d_%d" % (q, tt)])
                    if dbg:
                        S.dma("sp", dbg_x1[b, t0c + tt * 128: t0c + (tt + 1) * 128, :], x1[:, tt, :],
                              reads=["x1_%d_%d" % (q, tt)])
                ssq = nrm["ssq"]
                for tt in range(2):
                    S.op("act", lambda e, tt=tt: e.activation(
                        out=xsB[:, tt, :], in_=x1[:, tt, :], func=AF.Square, accum_out=ssq[:, tt:tt + 1]),
                        reads=["x1_%d_%d" % (q, tt)], writes=["xsB%d" % tt, "ssqB%d" % tt])
                    S.op("act", lambda e, tt=tt: e.activation(
                        out=nrm["lnv"][:, tt:tt + 1], in_=ssq[:, tt:tt + 1], func=AF.Ln,
                        scale=1.0 / D, bias=epsc[:, 0:1]), reads=["ssqB%d" % tt, "epsc"], writes=["lnvB%d" % tt])
                    S.op("act", lambda e, tt=tt: e.activation(
                        out=nrm["rstd"][:, tt:tt + 1], in_=nrm["lnv"][:, tt:tt + 1], func=AF.Exp, scale=-0.5),
                        reads=["lnvB%d" % tt], writes=["rstdB%d" % tt])
                    S.op("pool", lambda e, tt=tt: e.tensor_scalar(
                        out=xsB[:, tt, :], in0=x1[:, tt, :], scalar1=nrm["rstd"][:, tt:tt + 1], scalar2=0.0,
                        op0=ALU.mult, op1=ALU.add), reads=["x1_%d_%d" % (q, tt), "rstdB%d" % tt], writes=["xsB%d" % tt])
                for dr in range(4):
                    S.op("pe", [lambda e, dc=dc, tt=tt: e.transpose(
                        out=psT[:, dc % 2, tt * 128:(tt + 1) * 128], in_=xsB[:, tt, dc * 128:(dc + 1) * 128],
                        identity=IDENT) for dc in (2 * dr, 2 * dr + 1) for tt in range(2)],
                        reads=["xsB0", "xsB1", "kcb"], writes=["psT"])
                    S.op("dve", [lambda e, dc=dc: e.tensor_scalar(
                        out=h1T[:, dc, :], in0=psT[:, dc % 2, 0:TB],
                        scalar1=Acol[1][:, dc, b:b + 1], scalar2=Shcol[1][:, dc, b:b + 1],
                        op0=ALU.mult, op1=ALU.add) for dc in (2 * dr, 2 * dr + 1)],
                        reads=["psT", "Acol1", "Shcol1"], writes=["h1T%d" % (2 * dr), "h1T%d" % (2 * dr + 1)])

            def nloop(cb):
                h1k = ["h1T%d" % dc for dc in range(8)]
                def ctx(n):
                    p = n % 2
                    sfx = "_%d" % p
                    gb, gk = [(A[0], "A0"), (A[1], "A1"), (PJ, "PJ")][n % 3]
                    return dict(p=p, XB=Z[p], XK="Z%d" % p, GB=gb, GK=gk, RB=A[2 + p], RK="A%d" % (2 + p),
                                xbw=xbw2[p], xc=xc2[p], xcb=xcb2[p], sfx=sfx,
                                F=[t[:, :] for t in f32t2[p]], fk=["f%d%s" % (i, sfx) for i in range(6)])

                def stageA0(n):
                    c_ = ctx(n)
                    XB, XK, GB, GK = c_["XB"], c_["XK"], c_["GB"], c_["GK"]
                    for which, bank, bk in ((0, XB, XK), (1, GB, GK)):
                        S.op("pe", [lambda e, dc=dc: e.matmul(
                            bank[:, 0:TB], lhsT=wli[:, dc, which * 1024 + n * 128: which * 1024 + (n + 1) * 128],
                            rhs=h1T[:, dc, :], start=(dc == 0), stop=(dc == 7)) for dc in range(8)],
                            reads=h1k + ["wli"], writes=[bk])

                def stageA1(n):
                    c_ = ctx(n)
                    XB, XK, GB, GK, RB, RK = c_["XB"], c_["XK"], c_["GB"], c_["GK"], c_["RB"], c_["RK"]
                    xbw, xc, xcb, sfx = c_["xbw"], c_["xc"], c_["xcb"], c_["sfx"]
                    S.op("dve", [lambda e: e.tensor_copy(out=xbw[:, 0:3], in_=xbuf[:, n, 0:3]),
                                 lambda e: e.tensor_copy(out=xbw[:, 3:TB + 3], in_=XB[:, 0:TB])],
                         reads=[XK, "xbuf%d" % n, "xbuf"], writes=["xbw" + sfx])
                    S.op("dve", lambda e: e.tensor_copy(out=xbuf[:, n, 0:3], in_=xbw[:, TB:TB + 3]),
                         reads=["xbw" + sfx], writes=["xbuf%d" % n])
                    S.op("dve", lambda e: e.tensor_scalar(
                        out=xc[:, :], in0=xbw[:, 3:TB + 3], scalar1=CW(3, n), scalar2=CB(n),
                        op0=ALU.mult, op1=ALU.add), reads=["xbw" + sfx, "lcols"], writes=["xc" + sfx])
                    for k in range(3):
                        S.op("dve", lambda e: e.scalar_tensor_tensor(
                            out=xc[:, :], in0=xbw[:, k:k + TB], scalar=CW(k, n), in1=xc[:, :],
                            op0=ALU.mult, op1=ALU.add), reads=["xbw" + sfx, "xc" + sfx, "lcols"],
                            writes=["xc" + sfx])
                    S.op("pool", lambda e: e.tensor_copy(out=xcb[:, :], in_=xc[:, :]),
                         reads=["xc" + sfx], writes=["xcb" + sfx])
                    S.op("pe", [lambda e, which=which: e.matmul(
                        RB[:, which * TB:(which + 1) * TB], lhsT=wax[:, which, n, :], rhs=xcb[:, :],
                        start=True, stop=True) for which in range(2)],
                        reads=["xcb" + sfx, "wax"], writes=[RK])

                def stageA2(n):
                    c_ = ctx(n)
                    GB, GK, RB, RK = c_["GB"], c_["GK"], c_["RB"], c_["RK"]
                    xc, sfx, F, fk = c_["xc"], c_["sfx"], c_["F"], c_["fk"]
                    ch = [(RB[:, 0:TB], RK, NBA(n), F[0], F[1], fk[0], fk[1]),
                          (RB[:, TB:2 * TB], RK, NBX(n), F[2], F[4], fk[2], fk[4]),
                          (GB[:, 0:TB], GK, None, F[3], F[5], fk[3], fk[5])]
                    for (src_ap, src_key, nbias, t_a, t_b, ka, kb_) in ch:
                        if nbias is not None:
                            S.op("act", lambda e: e.activation(out=t_a, in_=src_ap, func=AF.Exp, scale=-1.0,
                                                               bias=nbias), reads=[src_key, "lder"], writes=[ka])
                        else:
                            S.op("act", lambda e: e.activation(out=t_a, in_=src_ap, func=AF.Exp, scale=-1.0),
                                 reads=[src_key], writes=[ka])
                    for (src_ap, src_key, nbias, t_a, t_b, ka, kb_) in ch:
                        S.op("act", lambda e: e.activation(out=t_b, in_=t_a, func=AF.Ln, bias=1.0),
                             reads=[ka], writes=[kb_])
                    for (src_ap, src_key, nbias, t_a, t_b, ka, kb_) in ch:
                        S.op("act", lambda e: e.activation(out=t_a, in_=t_b, func=AF.Exp, scale=-1.0),
                             reads=[kb_], writes=[ka])
                    S.op("act", lambda e: e.activation(out=F[1], in_=F[0], func=AF.Exp, scale=KC2(n)),
                         reads=[fk[0], "lder"], writes=[fk[1]])
                    S.op("act", lambda e: e.activation(out=F[4], in_=F[0], func=AF.Exp, scale=KC1(n)),
                         reads=[fk[0], "lder"], writes=[fk[4]])
                    S.op("act", lambda e: e.activation(out=F[5], in_=F[1], func=AF.Ln, scale=-1.0, bias=1.0),
                         reads=[fk[1]], writes=[fk[5]])
                    S.op("act", lambda e: e.activation(out=F[1], in_=F[5], func=AF.Exp, scale=0.5),
                         reads=[fk[5]], writes=[fk[1]])
                    S.op("pool", lambda e: e.tensor_tensor(out=F[2], in0=F[2], in1=xc[:, :], op=ALU.mult),
                         reads=[fk[2], "xc" + sfx], writes=[fk[2]])
                    S.op("dve", lambda e: e.tensor_tensor(out=F[3], in0=GB[:, 0:TB], in1=F[3], op=ALU.mult),
                         reads=[GK, fk[3]], writes=[fk[3]])

                def stageB(n):
                    c_ = ctx(n)
                    p, F, fk = c_["p"], c_["F"], c_["fk"]
                    S.op("pool", lambda e: e.tensor_tensor(out=F[2], in0=F[2], in1=F[1], op=ALU.mult),
                         reads=[fk[2], fk[1]], writes=[fk[2]])
                    S.op("dve", lambda e: e.tensor_tensor_scan(
                        out=F[5], data0=F[4], data1=F[2], initial=hst[:, n:n + 1], op0=ALU.mult, op1=ALU.add),
                        reads=[fk[4], fk[2], "hst%d" % n, "hst"], writes=[fk[5]])
                    S.op("dve", lambda e: e.tensor_copy(out=hst[:, n:n + 1], in_=f32t2[p][5][:, TB - 1:TB]),
                         reads=[fk[5]], writes=["hst%d" % n])
                    S.op("pool", lambda e: e.tensor_tensor(out=y1T[:, n, :], in0=F[5], in1=F[3], op=ALU.mult),
                         reads=[fk[5], fk[3]], writes=["y1T%d" % n])

                stageA0(0)
                stageA0(1)
                stageA1(0)
                stageA0(2)
                stageA1(1)
                stageA2(0)
                for n in range(8):
                    if n + 3 < 8:
                        stageA0(n + 3)
                    if n + 2 < 8:
                        stageA1(n + 2)
                    stageB(n)
                    if n + 1 < 8:
                        stageA2(n + 1)

            def stageO(cb):
                t0c = cb * TB
                q = cb % 2
                x1 = x1b[q]
                ssq = nrm["ssq"]
                y1k = ["y1T%d" % n for n in range(8)]
                if cb == 0:
                    scale_wo(1, wo1)
                for tt in range(2):
                    for hf in range(2):
                        bank = A[2 + hf]
                        bk = "A%d" % (2 + hf)
                        fns = []
                        for k in range(8):
                            fns.append(lambda e, k=k, tt=tt, hf=hf, bank=bank: e.matmul(
                                bank[:, :], lhsT=y1T[:, k, tt * 128:(tt + 1) * 128],
                                rhs=wo1[:, k, hf * 512:(hf + 1) * 512], start=(k == 0), stop=(k == 7)))
                        S.op("pe", fns, reads=y1k + ["wo1"], writes=[bk])
                        S.op("dve", lambda e, tt=tt, hf=hf, bank=bank: e.tensor_tensor(
                            out=x1[:, tt, hf * 512:(hf + 1) * 512], in0=bank[:, :],
                            in1=x1[:, tt, hf * 512:(hf + 1) * 512], op=ALU.add),
                            reads=[bk, "x1_%d_%d" % (q, tt)], writes=["x1_%d_%d" % (q, tt)])
                    oi = oti[0] % 2
                    oti[0] += 1
                    S.op("act", lambda e, tt=tt, oi=oi: e.activation(
                        out=outt[oi][:, :], in_=x1[:, tt, :], func=AF.Square, accum_out=ssq[:, 2 + tt:3 + tt]),
                        reads=["x1_%d_%d" % (q, tt)], writes=["outt%d" % oi, "ssqB%d" % (2 + tt)])
                    S.op("act", lambda e, tt=tt: e.activation(
                        out=nrm["lnv"][:, 2 + tt:3 + tt], in_=ssq[:, 2 + tt:3 + tt], func=AF.Ln,
                        scale=1.0 / D, bias=epsc[:, 0:1]), reads=["ssqB%d" % (2 + tt), "epsc"],
                        writes=["lnvB%d" % (2 + tt)])
                    S.op("act", lambda e, tt=tt: e.activation(
                        out=nrm["rstd"][:, 2 + tt:3 + tt], in_=nrm["lnv"][:, 2 + tt:3 + tt], func=AF.Exp,
                        scale=-0.5), reads=["lnvB%d" % (2 + tt)], writes=["rstdB%d" % (2 + tt)])
                    S.op("dve", lambda e, tt=tt, oi=oi: e.scalar_tensor_tensor(
                        out=outt[oi][:, :], in0=x1[:, tt, :], scalar=nrm["rstd"][:, 2 + tt:3 + tt], in1=gfbc[:, :],
                        op0=ALU.mult, op1=ALU.mult),
                        reads=["x1_%d_%d" % (q, tt), "rstdB%d" % (2 + tt), "gfbc"], writes=["outt%d" % oi])
                    S.dma("sp", out_d[b, t0c + tt * 128: t0c + (tt + 1) * 128, :], outt[oi][:, :],
                          reads=["outt%d" % oi])

            stageP(0)
            for cb in range(NTB):
                nloop(cb)
                if cb + 1 < NTB:
                    stageP(cb + 1)
                stageO(cb)
            S.barrier()
            AR.release(mB)

        S.wait_all("sp", S.all_events())
        S.emit()
        build.stats = dict(n_inst=S.n_inst, peak_sbuf=AR.peak,
                           per_eng={e: len(S.streams[e]) for e in S.ENG})
    return nc


_CACHE = {}


def make_in_maps(inputs, n_cores, nseq):
    kc, kf = host_consts()
    x = np.ascontiguousarray(inputs["x"], dtype=np.float32)
    c = np.asarray(inputs["c"], dtype=np.float32)
    pos = np.asarray(inputs["positions"]).astype(np.int32)
    lcols = np.zeros((128, 64), np.float32)
    cw = np.asarray(inputs["lru_conv_w"], np.float32)[0]
    for k in range(4):
        lcols[:, k * 8:(k + 1) * 8] = cw[k].reshape(8, 128).T
    for i, nm in enumerate(["lru_conv_b", "lru_b_a", "lru_b_x", "lru_lambda"]):
        lcols[:, 32 + i * 8: 40 + i * 8] = np.asarray(inputs[nm], np.float32)[0].reshape(8, 128).T
    shared = {
        "norm_g": np.asarray(inputs["norm_g"], np.float32),
        "w_mod": np.asarray(inputs["w_mod"], np.float32),
        "b_mod": np.asarray(inputs["b_mod"], np.float32),
        "attn_w_in": np.asarray(inputs["attn_w_in"], np.float32)[0],
        "attn_w_out": np.asarray(inputs["attn_w_out"], np.float32)[0],
        "lru_w_in": np.asarray(inputs["lru_w_in"], np.float32)[0],
        "lru_w_a": np.asarray(inputs["lru_w_a"], np.float32)[0],
        "lru_w_x": np.asarray(inputs["lru_w_x"], np.float32)[0],
        "lru_w_out": np.asarray(inputs["lru_w_out"], np.float32)[0],
        "lru_cols": lcols,
        "final_g": np.asarray(inputs["final_g"], np.float32).reshape(1, D),
        "kc": kc, "kf": kf,
    }
    maps = []
    for i in range(n_cores):
        sl = slice(i * nseq, (i + 1) * nseq)
        cc = c[sl]
        cT = np.ascontiguousarray(cc.reshape(nseq, 8, 128).transpose(2, 1, 0))
        m = dict(shared)
        m["x"] = x[sl]
        m["cT"] = cT
        m["pos"] = np.ascontiguousarray(pos[sl])
        maps.append(m)
    return maps


def kernel(**inputs):
    n_cores = 8
    B, S_LEN, _ = inputs["x"].shape
    nseq = B // n_cores
    key = (S_LEN, nseq)
    if key not in _CACHE:
        _CACHE[key] = build(S_LEN, nseq)
    nc = _CACHE[key]
    maps = make_in_maps(inputs, n_cores, nseq)
    res = run_bass_kernel_spmd(nc, maps, core_ids=list(range(n_cores)))
    out = np.concatenate([np.asarray(r["out"]) for r in res.results], axis=0)
    return out.astype(np.float32)
```

```python
import contextlib
import math
import numpy as np
import concourse.bass as bass
import concourse.mybir as mybir
from concourse.bass_utils import run_bass_kernel_spmd

F32 = mybir.dt.float32
BF16 = mybir.dt.bfloat16
I32 = mybir.dt.int32
U8 = mybir.dt.uint8
AF = mybir.ActivationFunctionType
ALU = mybir.AluOpType
AX = mybir.AxisListType

SEM_ROLL = 20000
N_DMA_SEMS = 24
D = 1024
NEG = -30000.0
EPS = 1e-6
TWO_PI = 2.0 * math.pi


class _Rec:
    def __init__(self):
        self.calls = []

    def __getattr__(self, name):
        def f(*a, **k):
            self.calls.append((name, a, k))
            return None
        return f


def _freeze(fns):
    out = []
    for fn in fns:
        r = _Rec()
        fn(r)
        assert len(r.calls) == 1, r.calls
        name, a, k = r.calls[0]
        out.append(lambda e, name=name, a=a, k=k: getattr(e, name)(*a, **k))
    return out


class Sched:
    ENG = ("pe", "act", "dve", "pool", "sp")

    def __init__(self, nc, stack):
        self.nc = nc
        self.stack = stack
        self.streams = {e: [] for e in self.ENG}
        self.sem = {}
        self.cnt = {}
        self.pe_sems = set()
        self.nsem = 0
        self.all_sems = []
        for e in self.ENG:
            self._new_eng_sem(e)
        self.dma_sems = [self._alloc_sem("dq%d" % i) for i in range(N_DMA_SEMS)]
        self.dma_val = [0] * N_DMA_SEMS
        self.dma_rr = 0
        self.sw_sems = []
        self.waited = {e: {} for e in self.ENG}
        self.last_w = {}
        self.readers = {}
        self.n_inst = 0

    def _alloc_sem(self, name):
        self.nsem += 1
        s = self.stack.enter_context(self.nc.semaphore("%s_%d" % (name, self.nsem)))
        return s

    def _new_eng_sem(self, e):
        self.sem[e] = self._alloc_sem("e_" + e)
        self.cnt[e] = 0
        self.all_sems.append([self.sem[e], 0])
        if e == "pe":
            self.pe_sems.add(id(self.sem[e]))

    def _need_waits(self, eng, events):
        best = {}
        for ev in events:
            if ev is None:
                continue
            s, v = ev
            k = id(s)
            if eng == "pe" and k in self.pe_sems:
                continue
            if self.waited[eng].get(k, 0) >= v:
                continue
            if k not in best or best[k][1] < v:
                best[k] = (s, v)
        out = []
        for k, (s, v) in best.items():
            self.waited[eng][k] = v
            out.append((s, v))
        return out

    EXCL = frozenset(["Z0", "Z1", "A0", "A1", "A2", "A3", "PJ", "psT"])

    def _deps(self, reads, writes):
        evs = []
        for k in reads:
            evs.append(self.last_w.get(k))
            if k in self.EXCL:
                evs.extend(self.readers.get(k, {}).values())
        for k in writes:
            evs.append(self.last_w.get(k))
            evs.extend(self.readers.get(k, {}).values())
        return evs

    def _commit(self, ev, reads, writes):
        for k in reads:
            d = self.readers.setdefault(k, {})
            d[id(ev[0])] = ev
        for k in writes:
            self.last_w[k] = ev
            self.readers[k] = {}

    def begin_defer(self):
        self._defer = []

    def end_defer(self):
        d, self._defer = self._defer, None
        return d

    def run_deferred(self, rec):
        kind = rec[0]
        if kind == "op":
            self.op(rec[1], rec[2], rec[3], rec[4], frozen=True)
        else:
            self.dma(rec[1], rec[2], rec[3], reads=rec[4], writes=rec[5], **rec[6])

    def op(self, eng, fns, reads=(), writes=(), extra=(), frozen=False):
        if callable(fns):
            fns = [fns]
        if not frozen:
            fns = _freeze(fns)
        if getattr(self, "_defer", None) is not None:
            self._defer.append(("op", eng, fns, tuple(reads), tuple(writes)))
            return None
        if self.cnt[eng] >= SEM_ROLL:
            self._new_eng_sem(eng)
        waits = self._need_waits(eng, self._deps(reads, writes) + list(extra))
        self.cnt[eng] += 1
        ev = (self.sem[eng], self.cnt[eng])
        self.streams[eng].append((waits, fns, ev, 1))
        self._commit(ev, reads, writes)
        self.n_inst += len(fns)
        return ev

    def dma(self, eng, out, in_, reads=(), writes=(), extra=(), **kw):
        if getattr(self, "_defer", None) is not None:
            self._defer.append(("dma", eng, out, in_, tuple(reads), tuple(writes), kw))
            return None
        if eng == "pool":
            s = self._alloc_sem("sw")
            self.sw_sems.append(s)
            waits = self._need_waits(eng, self._deps(reads, writes) + list(extra))
            ev = (s, 16)
        else:
            i = self.dma_rr
            self.dma_rr = (self.dma_rr + 1) % N_DMA_SEMS
            s = self.dma_sems[i]
            prev = (s, self.dma_val[i]) if self.dma_val[i] else None
            waits = self._need_waits(eng, self._deps(reads, writes) + list(extra) + [prev])
            self.dma_val[i] += 16
            ev = (s, self.dma_val[i])
        fn = lambda e, out=out, in_=in_, kw=kw: e.dma_start(out=out, in_=in_, **kw)
        self.streams[eng].append((waits, [fn], ev, 16))
        self._commit(ev, reads, writes)
        self.n_inst += 1
        return ev

    def all_events(self):
        evs = [(s, v) for s, v in zip(self.dma_sems, self.dma_val) if v]
        evs += [(s, 16) for s in self.sw_sems]
        for e in self.ENG:
            if self.cnt[e]:
                evs.append((self.sem[e], self.cnt[e]))
        return evs

    def barrier(self):
        evs = self.all_events()
        for e in self.ENG:
            waits = self._need_waits(e, evs)
            if waits:
                self.streams[e].append((waits, [], None, 0))

    def wait_all(self, eng, events):
        waits = self._need_waits(eng, events)
        if waits:
            self.streams[eng].append((waits, [], None, 0))

    def emit(self):
        nc = self.nc
        with nc.Block() as block:
            def run(eng_name):
                def body(e):
                    for waits, fns, ev, inc in self.streams[eng_name]:
                        for (s, v) in waits:
                            e.wait_ge(s, v)
                        for j, fn in enumerate(fns):
                            ins = fn(e)
                            if j == len(fns) - 1 and ev is not None:
                                ins.then_inc(ev[0], inc)
                return body
            block.tensor(run("pe"))
            block.scalar(run("act"))
            block.vector(run("dve"))
            block.gpsimd(run("pool"))
            block.sync(run("sp"))


class Arena:
    def __init__(self, nc, base, size):
        self.nc, self.base, self.size, self.off, self.n = nc, base, size, 0, 0
        self.peak = 0
        self.addr = {}

    def alloc(self, name, shape, dt):
        esz = {F32: 4, BF16: 2, I32: 4}[dt]
        nb = esz * int(np.prod(shape[1:]))
        nb = (nb + 63) // 64 * 64
        assert self.off + nb <= self.size, ("SBUF arena overflow", name, self.off, nb, self.size)
        self.n += 1
        h = self.nc.alloc_sbuf_tensor_at("%s_%d" % (name, self.n), list(shape), dt,
                                         offset=self.base + self.off)
        self.addr[id(h)] = self.base + self.off
        self.off += nb
        self.peak = max(self.peak, self.off)
        return h

    def alias(self, name, shape, dt, handle, byte_off=0):
        self.n += 1
        base = self.addr[id(handle)]
        h = self.nc.alloc_sbuf_tensor_at("%s_%d" % (name, self.n), list(shape), dt, offset=base + byte_off)
        self.addr[id(h)] = base + byte_off
        return h

    def mark(self):
        return self.off

    def release(self, m):
        self.off = m


K_IDENT, K_TRIM, K_NEGONES, K_MASKSB, K_MASKMB, K_PSWAP, K_ONES, K_ZERO = range(8)
KC_ROWSEL = 8 * 128
KC_COLS = 8 * 128 + 16 * 128


def host_consts():
    kc = np.zeros((128, KC_COLS), np.float32)
    p = np.arange(128)[:, None]
    j = np.arange(128)[None, :]
    kc[:, K_IDENT * 128:(K_IDENT + 1) * 128] = (p == j)
    kc[:, K_TRIM * 128:(K_TRIM + 1) * 128] = -1.0 * (p >= j)
    kc[:, K_NEGONES * 128:(K_NEGONES + 1) * 128] = -1.0
    kc[:, K_MASKSB * 128:(K_MASKSB + 1) * 128] = NEG * (p >= j)
    kc[:, K_MASKMB * 128:(K_MASKMB + 1) * 128] = NEG * (p > j)
    P = np.zeros((128, 128), np.float32)
    for hb in (0, 64):
        for i in range(8):
            P[hb + i + 8, hb + i] = -1.0
            P[hb + i, hb + 8 + i] = 1.0
    kc[:, K_PSWAP * 128:(K_PSWAP + 1) * 128] = P
    kc[:, K_ONES * 128:(K_ONES + 1) * 128] = 1.0
    for n in range(16):
        kc[n, KC_ROWSEL + n * 128: KC_ROWSEL + (n + 1) * 128] = 1.0
    kf = np.zeros((128, 128 + 128 + 2 + 256), np.float32)
    inv = 500000.0 ** (-np.arange(0, 16, 2, dtype=np.float32) / 16.0)
    for hb in (0, 64):
        for i in range(16):
            kf[0, hb + i] = inv[i % 8]
    kf[0, 128:256] = 1.0
    kf[0, 256] = 1.0
    kf[1, 257] = 1.0
    kf[0, 258:258 + 128] = 1.0
    kf[1, 258 + 128:258 + 256] = 1.0
    return kc, kf


def build(S_LEN=4096, NSEQ=2, dbg=False):
    NCH = S_LEN // 512
    NB128 = S_LEN // 128
    nc = bass.Bass("TRN2", target_bir_lowering=False)

    def din(name, shape, dt=F32):
        return nc.dram_tensor(name, list(shape), dt, kind="ExternalInput").ap()

    x_d = din("x", [NSEQ, S_LEN, D])
    cT_d = din("cT", [128, 8, NSEQ])
    pos_d = din("pos", [NSEQ, S_LEN], I32)
    ng_d = din("norm_g", [2, D])
    wmod_d = din("w_mod", [2, D, 3 * D])
    bmod_d = din("b_mod", [2, 3 * D])
    awin_d = din("attn_w_in", [D, 4096])
    awout_d = din("attn_w_out", [D, D])
    lwin_d = din("lru_w_in", [D, 2048])
    lwa_d = din("lru_w_a", [8, 128, 128])
    lwx_d = din("lru_w_x", [8, 128, 128])
    lwout_d = din("lru_w_out", [D, D])
    lcols_d = din("lru_cols", [128, 64])
    fg_d = din("final_g", [1, D])
    kc_d = din("kc", [128, KC_COLS])
    kf_d = din("kf", [128, 514])
    out_d = nc.dram_tensor("out", [NSEQ, S_LEN, D], F32, kind="ExternalOutput").ap()
    if dbg:
        dbg_y = nc.dram_tensor("dbg_y", [NSEQ, 128, 8, S_LEN], BF16, kind="ExternalOutput").ap()
        dbg_x1 = nc.dram_tensor("dbg_x1", [NSEQ, S_LEN, D], F32, kind="ExternalOutput").ap()

    with contextlib.ExitStack() as st:
        S = Sched(nc, st)
        fence = nc.alloc_sbuf_tensor("arena_fence", [128, 212800], U8)
        AR = Arena(nc, 16512, 212800)
        psb = [st.enter_context(nc.psum_tensor("psb%d" % i, [128, 512], F32)) for i in range(7)]
        psT = st.enter_context(nc.psum_tensor("psT", [128, 2, 512], BF16))
        Z = [psb[0], psb[1]]
        A = [psb[2], psb[3], psb[4], psb[5]]
        PJ = psb[6]

        kcb = AR.alloc("kcb", [128, KC_COLS], BF16)
        kfs = AR.alloc("kfs", [128, 514], F32)
        zrhs = AR.alloc("zrhs", [128, 512], BF16)
        Acol = [AR.alloc("Acol%d" % l, [128, 8, NSEQ], F32) for l in range(2)]
        Shcol = [AR.alloc("Shcol%d" % l, [128, 8, NSEQ], F32) for l in range(2)]
        grow = [AR.alloc("grow%d" % l, [NSEQ, D], F32) for l in range(2)]
        gfbc = AR.alloc("gfbc", [128, D], F32)
        lcols = AR.alloc("lcols", [128, 64], F32)
        lder = AR.alloc("lder", [128, 32], F32)
        yT = AR.alloc("yT", [128, 8, S_LEN], BF16)
        epsc = AR.alloc("epsc", [128, 1], F32)

        def C(kid, rows=128, c0=0, c1=128):
            return kcb[0:rows, kid * 128 + c0: kid * 128 + c1]

        IDENT = C(K_IDENT)
        TRIM = C(K_TRIM)
        NEGONES = C(K_NEGONES)
        MASKSB = C(K_MASKSB)
        MASKMB = C(K_MASKMB)
        PSWAP = C(K_PSWAP)
        ONES64 = C(K_ONES, 128, 0, 64)
        INVROW = kfs[0:1, 0:128]
        ONESROW = kfs[0:1, 128:256]
        I2 = kfs[0:NSEQ, 256:256 + NSEQ]

        def SEL2(b):
            return kfs[0:NSEQ, 258 + b * 128: 258 + (b + 1) * 128]

        for hf in range(2):
            S.dma("pool", kcb[:, hf * 1536:(hf + 1) * 1536], kc_d[:, hf * 1536:(hf + 1) * 1536], writes=["kcb"])
        S.dma("sp", kfs[:, :], kf_d, writes=["kfs"])
        S.dma("sp", lcols[:, :], lcols_d, writes=["lcols"])
        S.op("pool", lambda e: e.memset(zrhs[:, :], 0.0), writes=["zrhs"])
        S.op("pool", lambda e: e.memset(epsc[:, :], EPS), writes=["epsc"])

        m0 = AR.mark()
        cT = AR.alloc("cT", [128, 8, NSEQ], F32)
        modrow = [AR.alloc("modrow%d" % l, [NSEQ, 3 * D], F32) for l in range(2)]
        bmrow = [AR.alloc("bmrow%d" % l, [NSEQ, 3 * D], F32) for l in range(2)]
        grows = [AR.alloc("grows%d" % l, [NSEQ, D], F32) for l in range(2)]
        arow = [AR.alloc("arow%d" % l, [NSEQ, D], F32) for l in range(2)]
        fgrow = AR.alloc("fgrow", [1, D], F32)
        wstage = [AR.alloc("wstage%d" % i, [128, 8, 512], F32) for i in range(2)]
        S.dma("sp", cT[:, :, :], cT_d, writes=["cT"])
        S.dma("sp", fgrow[:, :], fg_d, writes=["fgrow"])
        for l in range(2):
            for b in range(NSEQ):
                S.dma("sp", bmrow[l][b:b + 1, :], bmod_d[l:l + 1, :], writes=["bmrow%d" % l])
                S.dma("sp", grows[l][b:b + 1, :], ng_d[l:l + 1, :], writes=["grows%d" % l])
        wm_v = [wmod_d[l].rearrange("(k p) f -> p k f", p=128) for l in range(2)]
        it = 0
        for l in range(2):
            for fg in range(6):
                ws = wstage[it % 2]
                wk = "wstage%d" % (it % 2)
                it += 1
                S.dma("sp", ws[:, :, :], wm_v[l][:, :, fg * 512:(fg + 1) * 512], writes=[wk])
                fns = []
                for k in range(8):
                    fns.append(lambda e, ws=ws, k=k: e.matmul(
                        PJ[0:NSEQ, :], lhsT=cT[:, k, :], rhs=ws[:, k, :], start=(k == 0), stop=(k == 7)))
                S.op("pe", fns, reads=["cT", wk], writes=["PJ"])
                S.op("dve", lambda e, l=l, fg=fg: e.tensor_tensor(
                    out=modrow[l][:, fg * 512:(fg + 1) * 512], in0=PJ[0:NSEQ, :],
                    in1=bmrow[l][:, fg * 512:(fg + 1) * 512], op=ALU.add),
                    reads=["PJ", "bmrow%d" % l], writes=["modrow%d" % l])
        for l in range(2):
            S.op("dve", lambda e, l=l: e.scalar_tensor_tensor(
                out=arow[l][:, :], in0=modrow[l][:, D:2 * D], scalar=1.0, in1=grows[l][:, :],
                op0=ALU.add, op1=ALU.mult), reads=["modrow%d" % l, "grows%d" % l], writes=["arow%d" % l])
            S.op("dve", lambda e, l=l: e.tensor_copy(out=grow[l][:, :], in_=modrow[l][:, 2 * D:3 * D]),
                 reads=["modrow%d" % l], writes=["grow%d" % l])
            fns = []
            for dc in range(8):
                fns.append(lambda e, l=l, dc=dc: e.matmul(
                    PJ[:, dc * NSEQ:(dc + 1) * NSEQ], lhsT=arow[l][:, dc * 128:(dc + 1) * 128],
                    rhs=I2, start=True, stop=True))
                fns.append(lambda e, l=l, dc=dc: e.matmul(
                    PJ[:, 64 + dc * NSEQ:64 + (dc + 1) * NSEQ], lhsT=modrow[l][:, dc * 128:(dc + 1) * 128],
                    rhs=I2, start=True, stop=True))
            S.op("pe", fns, reads=["arow%d" % l, "modrow%d" % l, "kfs"], writes=["PJ"])
            S.op("dve", lambda e, l=l: e.tensor_copy(
                out=Acol[l][:, :, :], in_=PJ[:, 0:8 * NSEQ].rearrange("p (a b) -> p a b", b=NSEQ)),
                reads=["PJ"], writes=["Acol%d" % l])
            S.op("dve", lambda e, l=l: e.tensor_copy(
                out=Shcol[l][:, :, :], in_=PJ[:, 64:64 + 8 * NSEQ].rearrange("p (a b) -> p a b", b=NSEQ)),
                reads=["PJ"], writes=["Shcol%d" % l])
        for hf in range(2):
            S.op("pe", lambda e, hf=hf: e.matmul(PJ[:, :], lhsT=ONESROW, rhs=fgrow[0:1, hf * 512:(hf + 1) * 512],
                                                 start=True, stop=True), reads=["kfs", "fgrow"], writes=["PJ"])
            S.op("dve", lambda e, hf=hf: e.tensor_copy(out=gfbc[:, hf * 512:(hf + 1) * 512], in_=PJ[:, :]),
                 reads=["PJ"], writes=["gfbc"])
        S.op("act", lambda e: e.activation(out=lder[:, 0:8], in_=lcols[:, 56:64], func=AF.Exp, scale=-1.0),
             reads=["lcols"], writes=["lder"])
        S.op("act", lambda e: e.activation(out=lder[:, 8:16], in_=lder[:, 0:8], func=AF.Ln, bias=1.0),
             reads=["lder"], writes=["lder"])
        S.op("dve", lambda e: e.tensor_scalar(out=lder[:, 0:8], in0=lder[:, 8:16], scalar1=-8.0, scalar2=None,
                                              op0=ALU.mult), reads=["lder"], writes=["lder"])
        S.op("dve", lambda e: e.tensor_scalar(out=lder[:, 8:16], in0=lder[:, 0:8], scalar1=2.0, scalar2=None,
                                              op0=ALU.mult), reads=["lder"], writes=["lder"])
        S.op("dve", lambda e: e.tensor_scalar(out=lder[:, 16:32], in0=lcols[:, 40:56], scalar1=-1.0, scalar2=None,
                                              op0=ALU.mult), reads=["lcols", "lder"], writes=["lder"])
        S.barrier()
        AR.release(m0)

        xv = x_d

        for b in range(NSEQ):
            mA = AR.mark()
            nrm = {"ssq": AR.alloc("ssq", [128, 4], F32), "lnv": AR.alloc("lnv", [128, 4], F32),
                   "rstd": AR.alloc("rstd", [128, 4], F32)}
            Wg = AR.alloc("Wg", [128, 8, 4, 256], BF16)
            kTc = AR.alloc("kTc", [128, 2, S_LEN], BF16)
            Vc = AR.alloc("Vc", [128, NB128, 256], BF16)
            hTc = AR.alloc("hT", [128, 8, 512], BF16)
            xt = [AR.alloc("xt%d" % i, [128, D], F32) for i in range(2)]
            xs = AR.alloc("xs", [128, 4, D], BF16)
            qT = [AR.alloc("qT%d" % i, [128, 2, 2, 512], BF16) for i in range(2)]
            sg = [AR.alloc("sg%d" % i, [128, 2, 512], BF16) for i in range(2)]
            negmT = [AR.alloc("negmT%d" % i, [128, 4, 512], BF16) for i in range(2)]
            Eb = [AR.alloc("Eb%d" % i, [128, 512], F32) for i in range(2)]
            Lb = [AR.alloc("Lb%d" % i, [128, 512], BF16) for i in range(2)]
            Ls = [AR.alloc("Ls%d" % i, [128, 512], BF16) for i in range(2)]
            Wt = [AR.alloc("Wt%d" % i, [128, 512], BF16) for i in range(2)]
            rtmp = AR.alloc("rtmp", [128, 512], BF16)
            t32 = [AR.alloc("t32_%d" % i, [128, 512], F32) for i in range(4)]
            ki = AR.alias("ki", [128, 512], I32, t32[3])
            posi = AR.alias("posi", [1, 512], I32, t32[2])
            posf = AR.alias("posf", [1, 512], F32, t32[1])
            q32 = AR.alloc("q32", [128, 2, 512], F32)
            kmean = AR.alloc("kmean", [128, 2, 16], F32)
            cs = [AR.alias("cossin0", [128, 512], F32, Lb[0]), AR.alias("cossin1", [128, 512], F32, Ls[0])]
            gp = AR.alias("gp", [128, 16, 16], F32, Eb[1])
            m8 = AR.alias("m8", [128, 16, 8], F32, Eb[1], 1024)
            negm = AR.alias("negm", [128, 16, 16], BF16, Eb[1], 1536)
            rden = AR.alias("rden", [128, 512], F32, Eb[0])
            xload_i = [0]
            for par in range(2):
                S.op("pool", lambda e, par=par: e.memset(qT[par][:, :, :, :], 0.0),
                     writes=["qT%d_%d" % (fc, par) for fc in range(2)])
                S.op("pool", lambda e, par=par: e.memset(negmT[par][:, :, :], 0.0),
                     writes=["negmT%d_%d" % (h, par) for h in range(4)])
            _kinds = ("sb", "mb")
            def load_Wg(kind, g):
                if kind == "sb":
                    cols = [0 + g * 256, 512 + g * 256, 1024 + g * 256, 3072 + g * 256]
                else:
                    cols = [1536 + g * 256, 2048 + g * 256, 2560 + g * 256, 3584 + g * 256]
                wv = awin_d.rearrange("(k p) c -> p k c", p=128)
                for j in range(4):
                    S.dma("pool", Wg[:, :, j, :], wv[:, :, cols[j]:cols[j] + 256], writes=["Wg"])

            passes = [(kind, g) for kind in _kinds for g in range(2)]
            wg_loaded = [False]
            for pi, (kind, g) in enumerate(passes):
                if True:
                    S.barrier()
                    if not wg_loaded[0]:
                        load_Wg(kind, g)
                    wg_loaded[0] = False
                    if kind == "mb":
                        S.op("pool", lambda e: e.memset(gp[:, :, :], -1e30), writes=["gp"])
                        S.op("pool", lambda e: e.memset(kmean[:, :, :], 0.0), writes=["kmean0", "kmean1"])
                    ych0 = (0 if kind == "sb" else 4) + 2 * g

                    def prep(c):
                        par = c % 2
                        qTp, sgp, nmp = qT[par], sg[par], negmT[par]
                        hkeys = ["hT%d" % dc for dc in range(8)]
                        if kind == "mb":
                            S.dma("sp", posi[:, :], pos_d[b:b + 1, c * 512:(c + 1) * 512], writes=["t32_2"])
                            S.op("dve", lambda e: e.tensor_copy(out=posf[:, :], in_=posi[:, :]),
                                 reads=["t32_2"], writes=["t32_1"])
                            S.op("pe", lambda e: e.matmul(PJ[:, :], lhsT=INVROW, rhs=posf[0:1, :],
                                                          start=True, stop=True),
                                 reads=["kfs", "t32_1"], writes=["PJ"])
                            S.op("dve", lambda e: e.tensor_scalar(out=ki[:, :], in0=PJ[:, :], scalar1=1.0 / TWO_PI,
                                                                  scalar2=None, op0=ALU.mult),
                                 reads=["PJ"], writes=["t32_3"])
                            S.op("dve", lambda e: e.tensor_copy(out=t32[0][:, :], in_=ki[:, :]),
                                 reads=["t32_3"], writes=["t32_0"])
                            S.op("dve", lambda e: e.scalar_tensor_tensor(
                                out=t32[1][:, :], in0=t32[0][:, :], scalar=-TWO_PI, in1=PJ[:, :],
                                op0=ALU.mult, op1=ALU.add), reads=["t32_0", "PJ"], writes=["t32_1"])
                            S.op("dve", lambda e: e.tensor_scalar(
                                out=t32[2][:, :], in0=t32[1][:, :], scalar1=math.pi, scalar2=-TWO_PI,
                                op0=ALU.is_gt, op1=ALU.mult), reads=["t32_1"], writes=["t32_2"])
                            S.op("dve", lambda e: e.tensor_tensor(
                                out=t32[2][:, :], in0=t32[2][:, :], in1=t32[1][:, :], op=ALU.add),
                                reads=["t32_1", "t32_2"], writes=["t32_2"])
                            S.op("dve", lambda e: e.tensor_scalar(
                                out=t32[1][:, :], in0=t32[1][:, :], scalar1=math.pi / 2, scalar2=None,
                                op0=ALU.add), reads=["t32_1"], writes=["t32_1"])
                            S.op("dve", lambda e: e.tensor_scalar(
                                out=t32[0][:, :], in0=t32[1][:, :], scalar1=math.pi, scalar2=-TWO_PI,
                                op0=ALU.is_gt, op1=ALU.mult), reads=["t32_1"], writes=["t32_0"])
                            S.op("dve", lambda e: e.tensor_tensor(
                                out=t32[0][:, :], in0=t32[0][:, :], in1=t32[1][:, :], op=ALU.add),
                                reads=["t32_1", "t32_0"], writes=["t32_0"])
                            S.op("act", [lambda e: e.activation(out=cs[0][:, :], in_=t32[2][:, :], func=AF.Sin),
                                         lambda e: e.activation(out=cs[1][:, :], in_=t32[0][:, :], func=AF.Sin)],
                                 reads=["t32_2", "t32_0"], writes=["cs0", "cs1"])

                        ssq = nrm["ssq"]
                        slots = {}

                        def xload(tt):
                            i = xload_i[0] % 2
                            xload_i[0] += 1
                            t0 = c * 512 + tt * 128
                            S.dma("sp", xt[i][:, :], xv[b, t0:t0 + 128, :], writes=["xt%d" % i])
                            slots[tt] = i
                        xload(0)
                        xload(1)
                        for tt in range(4):
                            i = slots[tt]
                            xk = "xt%d" % i
                            xa = xt[i][:, :]
                            S.op("dve", lambda e: e.scalar_tensor_tensor(
                                out=xs[:, tt, :], in0=xa, scalar=1.0, in1=xa, op0=ALU.mult, op1=ALU.mult,
                                accum_out=ssq[:, tt:tt + 1]),
                                reads=[xk], writes=["xs%d" % tt, "ssq%d" % tt])
                            S.op("act", lambda e: e.activation(
                                out=nrm["lnv"][:, tt:tt + 1], in_=ssq[:, tt:tt + 1], func=AF.Ln,
                                scale=1.0 / D, bias=epsc[:, 0:1]),
                                reads=["ssq%d" % tt, "epsc"], writes=["lnv%d" % tt])
                            S.op("act", lambda e: e.activation(
                                out=nrm["rstd"][:, tt:tt + 1], in_=nrm["lnv"][:, tt:tt + 1], func=AF.Exp,
                                scale=-0.5), reads=["lnv%d" % tt], writes=["rstd%d" % tt])
                            S.op("pool", lambda e: e.tensor_scalar(
                                out=xs[:, tt, :], in0=xa, scalar1=nrm["rstd"][:, tt:tt + 1], scalar2=0.0,
                                op0=ALU.mult, op1=ALU.add), reads=[xk, "rstd%d" % tt], writes=["xs%d" % tt])
                            if tt + 2 < 4:
                                xload(tt + 2)
                        for dr in range(4):
                            S.op("pe", [lambda e, tt=tt, dc=dc: e.transpose(
                                out=psT[:, dc % 2, tt * 128:(tt + 1) * 128],
                                in_=xs[:, tt, dc * 128:(dc + 1) * 128], identity=IDENT)
                                for dc in (2 * dr, 2 * dr + 1) for tt in range(4)],
                                reads=["xs%d" % t for t in range(4)] + ["kcb"], writes=["psT"])
                            S.op("dve", [lambda e, dc=dc: e.tensor_scalar(
                                out=hTc[:, dc, :], in0=psT[:, dc % 2, :],
                                scalar1=Acol[0][:, dc, b:b + 1], scalar2=Shcol[0][:, dc, b:b + 1],
                                op0=ALU.mult, op1=ALU.add) for dc in (2 * dr, 2 * dr + 1)],
                                reads=["psT", "Acol0", "Shcol0"], writes=["hT%d" % (2 * dr), "hT%d" % (2 * dr + 1)])

                        def proj_fm(j, fc):
                            S.op("pe", [lambda e, dc=dc: e.matmul(
                                PJ[:, :], lhsT=Wg[:, dc, j, fc * 128:(fc + 1) * 128], rhs=hTc[:, dc, :],
                                start=(dc == 0), stop=(dc == 7)) for dc in range(8)],
                                reads=hkeys + ["Wg"], writes=["PJ"])

                        def rope(dst32, dkey, scale):
                            S.op("act", lambda e: e.activation(out=rtmp[:, :], in_=PJ[:, :], func=AF.Copy,
                                                               scale=scale), reads=["PJ"], writes=["rtmp"])
                            S.op("dve", lambda e: e.scalar_tensor_tensor(
                                out=t32[0][:, :], in0=PJ[:, :], scalar=scale, in1=cs[1][:, :],
                                op0=ALU.mult, op1=ALU.mult), reads=["PJ", "cs1", "rtmp"], writes=["t32_0"])
                            S.op("pe", lambda e: e.matmul(PJ[:, :], lhsT=PSWAP, rhs=rtmp[:, :],
                                                          start=True, stop=True),
                                 reads=["rtmp", "kcb", "t32_0"], writes=["PJ"])
                            S.op("dve", lambda e: e.tensor_tensor(out=t32[1][:, :], in0=PJ[:, :],
                                                                  in1=cs[0][:, :], op=ALU.mult),
                                 reads=["PJ", "cs0"], writes=["t32_1"])
                            S.op("dve", lambda e: e.tensor_tensor(
                                out=dst32, in0=t32[0][:, :], in1=t32[1][:, :], op=ALU.add),
                                reads=["t32_0", "t32_1"], writes=[dkey])

                        for fc in range(2):
                            qk = "qT%d_%d" % (fc, par)
                            proj_fm(0, fc)
                            if kind == "sb":
                                S.op("dve", [lambda e, hh=hh: e.tensor_scalar(
                                    out=qTp[hh * 64:(hh + 1) * 64, fc, hh, :], in0=PJ[hh * 64:(hh + 1) * 64, :],
                                    scalar1=0.125, scalar2=None, op0=ALU.mult) for hh in range(2)],
                                    reads=["PJ"], writes=[qk])
                            else:
                                rope(q32[:, fc, :], "q32_%d" % fc, 0.125)
                                S.op("act", [lambda e, hh=hh: e.activation(
                                    out=qTp[hh * 64:(hh + 1) * 64, fc, hh, :], in_=q32[hh * 64:(hh + 1) * 64, fc, :],
                                    func=AF.Copy) for hh in range(2)], reads=["q32_%d" % fc], writes=[qk])
                            proj_fm(1, fc)
                            kkey = "kT%d_%d" % (fc, c)
                            if kind == "sb":
                                S.op("dve", lambda e: e.tensor_copy(
                                    out=kTc[:, fc, c * 512:(c + 1) * 512], in_=PJ[:, :]),
                                    reads=["PJ"], writes=[kkey])
                            else:
                                rope(t32[3][:, :], "t32_3", 1.0)
                                S.op("act", lambda e: e.activation(
                                    out=kTc[:, fc, c * 512:(c + 1) * 512], in_=t32[3][:, :], func=AF.Copy),
                                    reads=["t32_3"], writes=[kkey])
                                S.op("dve", lambda e: e.tensor_reduce(
                                    out=kmean[:, fc, 2 * c:2 * c + 2],
                                    in_=t32[3][:, :].rearrange("p (a b) -> p a b", a=2),
                                    axis=AX.X, op=ALU.add), reads=["t32_3"], writes=["kmean%d" % fc])
                                S.op("dve", lambda e: e.tensor_scalar(
                                    out=kmean[:, fc, 2 * c:2 * c + 2], in0=kmean[:, fc, 2 * c:2 * c + 2],
                                    scalar1=1.0 / 256.0, scalar2=None, op0=ALU.mult),
                                    reads=["kmean%d" % fc], writes=["kmean%d" % fc])
                            proj_fm(3, fc)
                            S.op("act", lambda e: e.activation(out=t32[0][:, :], in_=PJ[:, :], func=AF.Exp,
                                                               scale=-1.0), reads=["PJ"], writes=["t32_0"])
                            S.op("act", lambda e: e.activation(out=t32[1][:, :], in_=t32[0][:, :], func=AF.Ln,
                                                               bias=1.0), reads=["t32_0"], writes=["t32_1"])
                            S.op("act", lambda e: e.activation(out=t32[0][:, :], in_=t32[1][:, :], func=AF.Exp,
                                                               scale=-1.0), reads=["t32_1"], writes=["t32_0"])
                            S.op("dve", lambda e: e.tensor_tensor(
                                out=sgp[:, fc, :], in0=PJ[:, :], in1=t32[0][:, :], op=ALU.mult),
                                reads=["PJ", "t32_0"], writes=["sg%d_%d" % (fc, par)])
                        for tt in range(4):
                            S.op("pe", [lambda e, dc=dc: e.matmul(
                                PJ[:, 0:256], lhsT=hTc[:, dc, tt * 128:(tt + 1) * 128], rhs=Wg[:, dc, 2, :],
                                start=(dc == 0), stop=(dc == 7)) for dc in range(8)],
                                reads=hkeys + ["Wg"], writes=["PJ"])
                            S.op("dve", lambda e: e.tensor_copy(out=Vc[:, c * 4 + tt, :], in_=PJ[:, 0:256]),
                                 reads=["PJ"], writes=["V%d" % c])
                        if kind == "mb":
                            tts = [tt for tt in range(4) if 2 * c + (tt >= 2) > 0]
                            sls = [h * 4 + tt for h in range(4) for tt in tts]
                            S.op("pe", [lambda e, h=h, tt=tt: e.matmul(
                                PJ[:, (h * 4 + tt) * 16:(h * 4 + tt + 1) * 16],
                                lhsT=q32[(h % 2) * 64:(h % 2) * 64 + 64, h // 2, tt * 128:(tt + 1) * 128],
                                rhs=kmean[(h % 2) * 64:(h % 2) * 64 + 64, h // 2, 0:16], start=True, stop=True)
                                for h in range(4) for tt in tts],
                                reads=["q32_0", "q32_1", "kmean0", "kmean1"], writes=["PJ"])
                            pjv = PJ[:, 0:256].rearrange("p (h t n) -> p h t n", h=4, t=4)
                            gpv = gp[:, :, :].rearrange("p (h t) n -> p h t n", h=4)
                            fns = []
                            nv0, nv1 = min(2 * c, 16), min(2 * c + 1, 16)
                            if nv0 > 0:
                                fns.append(lambda e: e.tensor_copy(out=gpv[:, :, 0:2, 0:nv0], in_=pjv[:, :, 0:2, 0:nv0]))
                            fns.append(lambda e: e.tensor_copy(out=gpv[:, :, 2:4, 0:nv1], in_=pjv[:, :, 2:4, 0:nv1]))
                            S.op("dve", fns, reads=["PJ", "gp"], writes=["gpall"])
                            S.op("dve", [lambda e, sl=sl: e.max(out=m8[:, sl, :], in_=gp[:, sl, :]) for sl in sls],
                                 reads=["gp", "gpall"], writes=["m8all"])
                            S.op("dve", [lambda e, sl=sl: e.tensor_scalar(
                                out=negm[:, sl, :], in0=gp[:, sl, :], scalar1=m8[:, sl, 2:3], scalar2=NEG,
                                op0=ALU.is_lt, op1=ALU.mult) for sl in sls],
                                reads=["gp", "gpall", "m8all"], writes=["negmall"])
                            c0 = tts[0] * 128
                            for hp in range(2):
                                S.op("pe", [lambda e, hh=hh, tt=tt: e.transpose(
                                    out=psT[0:16, hh, tt * 128:(tt + 1) * 128], in_=negm[:, (2 * hp + hh) * 4 + tt, :],
                                    identity=IDENT) for hh in range(2) for tt in tts],
                                    reads=["negmall", "kcb"], writes=["psT"])
                                S.op("dve", lambda e: e.tensor_copy(
                                    out=nmp[0:16, 2 * hp:2 * hp + 2, c0:512], in_=psT[0:16, 0:2, c0:512]),
                                    reads=["psT"], writes=["negmT%d_%d" % (2 * hp, par), "negmT%d_%d" % (2 * hp + 1, par)])

                    def sb_tiles(c, pump):
                        par = c % 2
                        qTp, sgp = qT[par], sg[par]
                        tiles = []
                        for h in range(4):
                            kbs = list(range(4 * c + 3, -1, -1))
                            for i, kb in enumerate(kbs):
                                j = kb - 4 * c
                                lo = 128 * j if j >= 0 else 0
                                tiles.append(dict(h=h, kb=kb, lo=lo, diag=(j >= 0), first=(i == 0),
                                                  last=(kb == 0), idx=len(tiles)))
                        for i, T in enumerate(tiles):
                            T["lo_prev"] = None if T["first"] else tiles[i - 1]["lo"]

                        def qk_fn(T, dst, start, stop):
                            h, kb, lo = T["h"], T["kb"], T["lo"]
                            fc = h // 2
                            return lambda e: e.matmul(dst[:, lo:512], lhsT=kTc[:, fc, kb * 128:(kb + 1) * 128],
                                                      rhs=qTp[:, fc, h % 2, lo:512], start=start, stop=stop)

                        def emit_Z(T):
                            zb = T["idx"] % 2
                            h, kb, lo = T["h"], T["kb"], T["lo"]
                            fc = h // 2
                            fns = [qk_fn(T, Z[zb], True, not T["diag"])]
                            if T["diag"]:
                                fns.append(lambda e: e.matmul(Z[zb][:, lo:lo + 128], lhsT=IDENT, rhs=MASKSB,
                                                              start=False, stop=True))
                            S.op("pe", fns, reads=["kT%d_%d" % (fc, kb // 4), "qT%d_%d" % (fc, par), "kcb"],
                                 writes=["Z%d" % zb])

                        def emit_E(T):
                            zb = T["idx"] % 2
                            lo = T["lo"]
                            S.op("act", lambda e: e.activation(out=Eb[zb][:, lo:512], in_=Z[zb][:, lo:512],
                                                               func=AF.Exp),
                                 reads=["Z%d" % zb], writes=["Eb%d" % zb])

                        def emit_L(T):
                            zb = T["idx"] % 2
                            lo = T["lo"]
                            S.op("act", lambda e: e.activation(out=Lb[zb][:, lo:512], in_=Eb[zb][:, lo:512],
                                                               func=AF.Ln, bias=1.0),
                                 reads=["Eb%d" % zb], writes=["Lb%d" % zb])

                        def emit_B(T):
                            zb = T["idx"] % 2
                            h, kb, lo = T["h"], T["kb"], T["lo"]
                            fc = h // 2
                            lso = (T["idx"] + 1) % 2
                            fns = [lambda e: e.matmul(A[zb][:, lo:512], lhsT=TRIM, rhs=Lb[zb][:, lo:512],
                                                      start=True, stop=False)]
                            rd = ["Lb%d" % zb, "kT%d_%d" % (fc, kb // 4), "qT%d_%d" % (fc, par), "kcb"]
                            if not T["first"]:
                                lp = T["lo_prev"]
                                fns.append(lambda e: e.matmul(A[zb][:, lp:512], lhsT=NEGONES,
                                                              rhs=Ls[lso][:, lp:512], start=False, stop=False))
                                rd.append("Ls%d" % lso)
                            fns.append(qk_fn(T, A[zb], False, not T["diag"]))
                            if T["diag"]:
                                fns.append(lambda e: e.matmul(A[zb][:, lo:lo + 128], lhsT=IDENT, rhs=MASKSB,
                                                              start=False, stop=True))
                            S.op("pe", fns, reads=rd, writes=["A%d" % zb])

                        def emit_LsUpd(T):
                            if T["last"]:
                                return
                            zb = T["idx"] % 2
                            lo = T["lo"]
                            lsn = T["idx"] % 2
                            lso = (T["idx"] + 1) % 2
                            if T["first"]:
                                S.op("dve", lambda e: e.tensor_copy(out=Ls[lsn][:, lo:512], in_=Lb[zb][:, lo:512]),
                                     reads=["Lb%d" % zb], writes=["Ls%d" % lsn])
                            else:
                                lp = T["lo_prev"]
                                fns = []
                                if lo < lp:
                                    fns.append(lambda e: e.tensor_copy(out=Ls[lsn][:, lo:lp], in_=Lb[zb][:, lo:lp]))
                                fns.append(lambda e: e.tensor_tensor(
                                    out=Ls[lsn][:, lp:512], in0=Ls[lso][:, lp:512], in1=Lb[zb][:, lp:512],
                                    op=ALU.add))
                                S.op("dve", fns, reads=["Lb%d" % zb, "Ls%d" % lso], writes=["Ls%d" % lsn])

                        def emit_W(T):
                            zb = T["idx"] % 2
                            lo = T["lo"]
                            S.op("act", lambda e: e.activation(out=Wt[zb][:, lo:512], in_=A[zb][:, lo:512],
                                                               func=AF.Exp),
                                 reads=["A%d" % zb], writes=["Wt%d" % zb])

                        def emit_PV(T):
                            zb = T["idx"] % 2
                            h, kb, lo = T["h"], T["kb"], T["lo"]
                            fc, hh = h // 2, h % 2
                            O = A[2 + hh]
                            ok = "A%d" % (2 + hh)
                            fns = []
                            if T["first"]:
                                fns.append(lambda e: e.matmul(O[:, :], lhsT=IDENT, rhs=zrhs[:, :],
                                                              start=True, stop=False))
                            fns.append(lambda e: e.matmul(O[:, lo:512], lhsT=Vc[:, kb, fc * 128:(fc + 1) * 128],
                                                          rhs=Wt[zb][:, lo:512], start=False, stop=T["last"]))
                            S.op("pe", fns, reads=["Wt%d" % zb, "V%d" % (kb // 4), "zrhs", "kcb"], writes=[ok])
                            if T["last"]:
                                r0, r1 = hh * 64, (hh + 1) * 64
                                S.op("dve", lambda e: e.tensor_tensor(
                                    out=yT[r0:r1, ych0 + fc, c * 512:(c + 1) * 512], in0=O[r0:r1, :],
                                    in1=sgp[r0:r1, fc, :], op=ALU.mult),
                                    reads=[ok, "sg%d_%d" % (fc, par)], writes=["yT"])

                        n = len(tiles)
                        emit_Z(tiles[0])
                        if n > 1:
                            emit_Z(tiles[1])
                        emit_E(tiles[0])
                        for i in range(n):
                            if i + 2 < n:
                                emit_Z(tiles[i + 2])
                            if i + 1 < n:
                                emit_E(tiles[i + 1])
                            emit_L(tiles[i])
                            emit_B(tiles[i])
                            emit_LsUpd(tiles[i])
                            if i >= 1:
                                emit_W(tiles[i - 1])
                                emit_PV(tiles[i - 1])
                            pump(i, n)
                        emit_W(tiles[n - 1])
                        emit_PV(tiles[n - 1])

                    def mb_tiles(c, pump):
                        par = c % 2
                        qTp, sgp, nmp = qT[par], sg[par], negmT[par]
                        tiles = []
                        for h in range(4):
                            kbs = list(range(0, 4 * c + 4))
                            for i, kb in enumerate(kbs):
                                j = kb - 4 * c
                                lo = 128 * j if j >= 0 else 0
                                tiles.append(dict(h=h, kb=kb, lo=lo, j=j, first=(i == 0),
                                                  last=(i == len(kbs) - 1), idx=len(tiles)))

                        def emit_Zm(T):
                            zb = T["idx"] % 2
                            h, kb, lo, j = T["h"], T["kb"], T["lo"], T["j"]
                            fc = h // 2
                            nblk = kb // 2
                            extra = []
                            if j < 0:
                                extra.append((0, 512, "sel"))
                            elif j in (0, 1):
                                extra.append((256, 512, "sel"))
                            if j >= 0:
                                extra.append((lo, lo + 128, "tri"))
                            fns = [lambda e: e.matmul(
                                Z[zb][:, lo:512], lhsT=kTc[:, fc, kb * 128:(kb + 1) * 128],
                                rhs=qTp[:, fc, h % 2, lo:512], start=True, stop=False)]
                            for ei, (a0, a1, kd) in enumerate(extra):
                                lastm = (ei == len(extra) - 1)
                                if kd == "sel":
                                    fns.append(lambda e, a0=a0, a1=a1, lastm=lastm: e.matmul(
                                        Z[zb][:, a0:a1],
                                        lhsT=kcb[:, KC_ROWSEL + nblk * 128: KC_ROWSEL + (nblk + 1) * 128],
                                        rhs=nmp[:, h, a0:a1], start=False, stop=lastm))
                                else:
                                    fns.append(lambda e, a0=a0, a1=a1, lastm=lastm: e.matmul(
                                        Z[zb][:, a0:a1], lhsT=IDENT, rhs=MASKMB, start=False, stop=lastm))
                            S.op("pe", fns, reads=["kT%d_%d" % (fc, kb // 4), "qT%d_%d" % (fc, par), "kcb",
                                                   "negmT%d_%d" % (h, par)], writes=["Z%d" % zb])

                        def emit_Wm(T):
                            zb = T["idx"] % 2
                            lo = T["lo"]
                            S.op("act", lambda e: e.activation(out=Wt[zb][:, lo:512], in_=Z[zb][:, lo:512],
                                                               func=AF.Exp),
                                 reads=["Z%d" % zb], writes=["Wt%d" % zb])

                        def emit_PVm(T):
                            zb = T["idx"] % 2
                            h, kb, lo = T["h"], T["kb"], T["lo"]
                            fc, hh = h // 2, h % 2
                            O, Dn = A[hh], A[2 + hh]
                            ok = ["A%d" % hh, "A%d" % (2 + hh)]
                            fns = [lambda e: e.matmul(O[:, lo:512], lhsT=Vc[:, kb, fc * 128:(fc + 1) * 128],
                                                      rhs=Wt[zb][:, lo:512], start=T["first"], stop=T["last"]),
                                   lambda e: e.matmul(Dn[:, lo:512], lhsT=C(K_ONES),
                                                      rhs=Wt[zb][:, lo:512], start=T["first"], stop=T["last"])]
                            S.op("pe", fns, reads=["Wt%d" % zb, "V%d" % (kb // 4), "kcb"], writes=ok)
                            if T["last"]:
                                r0, r1 = hh * 64, (hh + 1) * 64
                                rk = "rden%d" % hh
                                S.op("act", lambda e: e.activation(out=rden[r0:r1, :], in_=Dn[r0:r1, :],
                                                                   func=AF.Ln), reads=ok, writes=[rk])
                                S.op("act", lambda e: e.activation(out=rden[r0:r1, :], in_=rden[r0:r1, :],
                                                                   func=AF.Exp, scale=-1.0), reads=[rk], writes=[rk])
                                S.op("dve", lambda e: e.tensor_tensor(out=rden[r0:r1, :], in0=O[r0:r1, :],
                                                                      in1=rden[r0:r1, :], op=ALU.mult),
                                     reads=ok + [rk], writes=[rk])
                                S.op("dve", lambda e: e.tensor_tensor(
                                    out=yT[r0:r1, ych0 + fc, c * 512:(c + 1) * 512], in0=rden[r0:r1, :],
                                    in1=sgp[r0:r1, fc, :], op=ALU.mult),
                                    reads=[rk, "sg%d_%d" % (fc, par)], writes=["yT"])

                        n = len(tiles)
                        emit_Zm(tiles[0])
                        for i in range(n):
                            emit_Wm(tiles[i])
                            if i + 1 < n:
                                emit_Zm(tiles[i + 1])
                            emit_PVm(tiles[i])
                            pump(i, n)

                    prep(0)
                    for c in range(NCH):
                        if c + 1 < NCH:
                            S.begin_defer()
                            prep(c + 1)
                            pend = S.end_defer()
                        else:
                            pend = []
                        state = {"k": 0}

                        def pump(i, n, pend=pend, state=state):
                            span = max(1, int(n * 0.95))
                            target = len(pend) if i + 1 >= span else (len(pend) * (i + 1)) // span
                            while state["k"] < target:
                                S.run_deferred(pend[state["k"]])
                                state["k"] += 1
                        if c == NCH - 1 and pi + 1 < len(passes):
                            load_Wg(*passes[pi + 1])
                            wg_loaded[0] = True
                        if kind == "sb":
                            sb_tiles(c, pump)
                        else:
                            mb_tiles(c, pump)
                        while state["k"] < len(pend):
                            S.run_deferred(pend[state["k"]])
                            state["k"] += 1
            S.barrier()
            AR.release(mA)
            if dbg:
                S.dma("sp", dbg_y[b], yT[:, :, :], reads=["yT"])

            mB = AR.mark()
            TB = 256
            NTB = S_LEN // TB
            nrm = {"ssq": AR.alloc("ssqB", [128, 4], F32), "lnv": AR.alloc("lnvB", [128, 4], F32),
                   "rstd": AR.alloc("rstdB", [128, 4], F32)}
            wo0 = AR.alloc("wo0", [128, 8, D], BF16)
            wo1 = AR.alloc("wo1", [128, 8, D], BF16)
            wli = AR.alloc("wli", [128, 8, 2048], BF16)
            wax = AR.alloc("wax", [128, 2, 8, 128], BF16)
            x1b = [AR.alloc("x1_%d" % i, [128, 2, D], F32) for i in range(2)]
            xsB = AR.alloc("xsB", [128, 2, D], BF16)
            h1T = AR.alloc("h1T", [128, 8, TB], BF16)
            y1T = AR.alloc("y1T", [128, 8, TB], BF16)
            xbuf = AR.alloc("xbuf", [128, 8, 4], F32)
            xbw2 = [AR.alloc("xbw%d" % i, [128, TB + 3], F32) for i in range(2)]
            xc2 = [AR.alloc("xc%d" % i, [128, TB], F32) for i in range(2)]
            xcb2 = [AR.alloc("xcb%d" % i, [128, TB], BF16) for i in range(2)]
            f32t2 = [[AR.alloc("f32t%d_%d" % (p, i), [128, TB], F32) for i in range(6)] for p in range(2)]
            hst = AR.alloc("hst", [128, 8], F32)
            outt = [AR.alloc("outt%d" % i, [128, D], F32) for i in range(2)]

            S.dma("pool", wo0[:, :, :], awout_d.rearrange("(k p) c -> p k c", p=128), writes=["wo0"])
            for hf in range(2):
                S.dma("pool", wli[:, :, hf * 1024:(hf + 1) * 1024],
                      lwin_d.rearrange("(k p) c -> p k c", p=128)[:, :, hf * 1024:(hf + 1) * 1024],
                      writes=["wli"])
            S.dma("pool", wax[:, 0, :, :], lwa_d.rearrange("n c d -> c n d"), writes=["wax"])
            S.dma("pool", wax[:, 1, :, :], lwx_d.rearrange("n c d -> c n d"), writes=["wax"])
            S.dma("pool", wo1[:, :, :], lwout_d.rearrange("(k p) c -> p k c", p=128), writes=["wo1"])

            def scale_wo(l, wo):
                for hf in range(2):
                    S.op("pe", lambda e, l=l, hf=hf: e.matmul(
                        PJ[:, :], lhsT=SEL2(b), rhs=grow[l][:, hf * 512:(hf + 1) * 512], start=True, stop=True),
                        reads=["kfs", "grow%d" % l], writes=["PJ"])
                    for k in range(8):
                        S.op("dve", lambda e, wo=wo, k=k, hf=hf: e.tensor_tensor(
                            out=wo[:, k, hf * 512:(hf + 1) * 512], in0=PJ[:, :],
                            in1=wo[:, k, hf * 512:(hf + 1) * 512], op=ALU.mult),
                            reads=["PJ", "wo%d" % l], writes=["wo%d" % l])
            scale_wo(0, wo0)
            S.op("pool", lambda e: e.memset(xbuf[:, :, :], 0.0), writes=["xbuf"])
            S.op("pool", lambda e: e.memset(hst[:, :], 0.0), writes=["hst"])

            CW = lambda k, n: lcols[:, k * 8 + n: k * 8 + n + 1]
            CB = lambda n: lcols[:, 32 + n: 33 + n]
            KC1 = lambda n: lder[:, n:n + 1]
            KC2 = lambda n: lder[:, 8 + n: 9 + n]
            NBA = lambda n: lder[:, 16 + n: 17 + n]
            NBX = lambda n: lder[:, 24 + n: 25 + n]
            xli = [0]
            oti = [0]
            def stageP(cb):
                t0c = cb * TB
                q = cb % 2
                x1 = x1b[q]
                for tt in range(2):
                    S.dma("sp", x1[:, tt, :], xv[b, t0c + tt * 128: t0c + (tt + 1) * 128, :],
                          writes=["x1_%d_%d" % (q, tt)])
                for tt in range(2):
                    for hf in range(2):
                        bank = A[(tt * 2 + hf) % 4]
                        bk = "A%d" % ((tt * 2 + hf) % 4)
                        S.op("pe", [lambda e, k=k: e.matmul(
                            bank[:, :], lhsT=yT[:, k, t0c + tt * 128: t0c + (tt + 1) * 128],
                            rhs=wo0[:, k, hf * 512:(hf + 1) * 512], start=(k == 0), stop=(k == 7))
                            for k in range(8)], reads=["yT", "wo0"], writes=[bk])
                        S.op("dve", lambda e: e.tensor_tensor(
                            out=x1[:, tt, hf * 512:(hf + 1) * 512], in0=bank[:, :],
                            in1=x1[:, tt, hf * 512:(hf + 1) * 512], op=ALU.add),
                            reads=[bk, "x1_%d_%d" % (q, tt)], writes=["x1_%d_%d" % (q, tt)])
                    if dbg:
                        S.dma("sp", dbg_x1[b, t0c + tt * 128: t0c + (tt + 1) * 128, :], x1[:, tt, :],
                              reads=["x1_%d_%d" % (q, tt)])
                ssq = nrm["ssq"]
                for tt in range(2):
                    S.op("act", lambda e, tt=tt: e.activation(
                        out=xsB[:, tt, :], in_=x1[:, tt, :], func=AF.Square, accum_out=ssq[:, tt:tt + 1]),
                        reads=["x1_%d_%d" % (q, tt)], writes=["xsB%d" % tt, "ssqB%d" % tt])
                    S.op("act", lambda e, tt=tt: e.activation(
                        out=nrm["lnv"][:, tt:tt + 1], in_=ssq[:, tt:tt + 1], func=AF.Ln,
                        scale=1.0 / D, bias=epsc[:, 0:1]), reads=["ssqB%d" % tt, "epsc"], writes=["lnvB%d" % tt])
                    S.op("act", lambda e, tt=tt: e.activation(
                        out=nrm["rstd"][:, tt:tt + 1], in_=nrm["lnv"][:, tt:tt + 1], func=AF.Exp, scale=-0.5),
                        reads=["lnvB%d" % tt], writes=["rstdB%d" % tt])
                    S.op("pool", lambda e, tt=tt: e.tensor_scalar(
                        out=xsB[:, tt, :], in0=x1[:, tt, :], scalar1=nrm["rstd"][:, tt:tt + 1], scalar2=0.0,
                        op0=ALU.mult, op1=ALU.add), reads=["x1_%d_%d" % (q, tt), "rstdB%d" % tt], writes=["xsB%d" % tt])
                for dr in range(4):
                    S.op("pe", [lambda e, dc=dc, tt=tt: e.transpose(
                        out=psT[:, dc % 2, tt * 128:(tt + 1) * 128], in_=xsB[:, tt, dc * 128:(dc + 1) * 128],
                        identity=IDENT) for dc in (2 * dr, 2 * dr + 1) for tt in range(2)],
                        reads=["xsB0", "xsB1", "kcb"], writes=["psT"])
                    S.op("dve", [lambda e, dc=dc: e.tensor_scalar(
                        out=h1T[:, dc, :], in0=psT[:, dc % 2, 0:TB],
                        scalar1=Acol[1][:, dc, b:b + 1], scalar2=Shcol[1][:, dc, b:b + 1],
                        op0=ALU.mult, op1=ALU.add) for dc in (2 * dr, 2 * dr + 1)],
                        reads=["psT", "Acol1", "Shcol1"], writes=["h1T%d" % (2 * dr), "h1T%d" % (2 * dr + 1)])

            def nloop(cb):
                h1k = ["h1T%d" % dc for dc in range(8)]
                def ctx(n):
                    p = n % 2
                    sfx = "_%d" % p
                    gb, gk = [(A[0], "A0"), (A[1], "A1"), (PJ, "PJ")][n % 3]
                    return dict(p=p, XB=Z[p], XK="Z%d" % p, GB=gb, GK=gk, RB=A[2 + p], RK="A%d" % (2 + p),
                                xbw=xbw2[p], xc=xc2[p], xcb=xcb2[p], sfx=sfx,
                                F=[t[:, :] for t in f32t2[p]], fk=["f%d%s" % (i, sfx) for i in range(6)])

                def stageA0(n):
                    c_ = ctx(n)
                    XB, XK, GB, GK = c_["XB"], c_["XK"], c_["GB"], c_["GK"]
                    for which, bank, bk in ((0, XB, XK), (1, GB, GK)):
                        S.op("pe", [lambda e, dc=dc: e.matmul(
                            bank[:, 0:TB], lhsT=wli[:, dc, which * 1024 + n * 128: which * 1024 + (n + 1) * 128],
                            rhs=h1T[:, dc, :], start=(dc == 0), stop=(dc == 7)) for dc in range(8)],
                            reads=h1k + ["wli"], writes=[bk])

                def stageA1(n):
                    c_ = ctx(n)
                    XB, XK, GB, GK, RB, RK = c_["XB"], c_["XK"], c_["GB"], c_["GK"], c_["RB"], c_["RK"]
                    xbw, xc, xcb, sfx = c_["xbw"], c_["xc"], c_["xcb"], c_["sfx"]
                    S.op("dve", [lambda e: e.tensor_copy(out=xbw[:, 0:3], in_=xbuf[:, n, 0:3]),
                                 lambda e: e.tensor_copy(out=xbw[:, 3:TB + 3], in_=XB[:, 0:TB])],
                         reads=[XK, "xbuf%d" % n, "xbuf"], writes=["xbw" + sfx])
                    S.op("dve", lambda e: e.tensor_copy(out=xbuf[:, n, 0:3], in_=xbw[:, TB:TB + 3]),
                         reads=["xbw" + sfx], writes=["xbuf%d" % n])
                    S.op("dve", lambda e: e.tensor_scalar(
                        out=xc[:, :], in0=xbw[:, 3:TB + 3], scalar1=CW(3, n), scalar2=CB(n),
                        op0=ALU.mult, op1=ALU.add), reads=["xbw" + sfx, "lcols"], writes=["xc" + sfx])
                    for k in range(3):
                        S.op("dve", lambda e: e.scalar_tensor_tensor(
                            out=xc[:, :], in0=xbw[:, k:k + TB], scalar=CW(k, n), in1=xc[:, :],
                            op0=ALU.mult, op1=ALU.add), reads=["xbw" + sfx, "xc" + sfx, "lcols"],
                            writes=["xc" + sfx])
                    S.op("pool", lambda e: e.tensor_copy(out=xcb[:, :], in_=xc[:, :]),
                         reads=["xc" + sfx], writes=["xcb" + sfx])
                    S.op("pe", [lambda e, which=which: e.matmul(
                        RB[:, which * TB:(which + 1) * TB], lhsT=wax[:, which, n, :], rhs=xcb[:, :],
                        start=True, stop=True) for which in range(2)],
                        reads=["xcb" + sfx, "wax"], writes=[RK])

                def stageA2(n):
                    c_ = ctx(n)
                    GB, GK, RB, RK = c_["GB"], c_["GK"], c_["RB"], c_["RK"]
                    xc, sfx, F, fk = c_["xc"], c_["sfx"], c_["F"], c_["fk"]
                    ch = [(RB[:, 0:TB], RK, NBA(n), F[0], F[1], fk[0], fk[1]),
                          (RB[:, TB:2 * TB], RK, NBX(n), F[2], F[4], fk[2], fk[4]),
                          (GB[:, 0:TB], GK, None, F[3], F[5], fk[3], fk[5])]
                    for (src_ap, src_key, nbias, t_a, t_b, ka, kb_) in ch:
                        if nbias is not None:
                            S.op("act", lambda e: e.activation(out=t_a, in_=src_ap, func=AF.Exp, scale=-1.0,
                                                               bias=nbias), reads=[src_key, "lder"], writes=[ka])
                        else:
                            S.op("act", lambda e: e.activation(out=t_a, in_=src_ap, func=AF.Exp, scale=-1.0),
                                 reads=[src_key], writes=[ka])
                    for (src_ap, src_key, nbias, t_a, t_b, ka, kb_) in ch:
                        S.op("act", lambda e: e.activation(out=t_b, in_=t_a, func=AF.Ln, bias=1.0),
                             reads=[ka], writes=[kb_])
                    for (src_ap, src_key, nbias, t_a, t_b, ka, kb_) in ch:
                        S.op("act", lambda e: e.activation(out=t_a, in_=t_b, func=AF.Exp, scale=-1.0),
                             reads=[kb_], writes=[ka])
                    S.op("act", lambda e: e.activation(out=F[1], in_=F[0], func=AF.Exp, scale=KC2(n)),
                         reads=[fk[0], "lder"], writes=[fk[1]])
                    S.op("act", lambda e: e.activation(out=F[4], in_=F[0], func=AF.Exp, scale=KC1(n)),
                         reads=[fk[0], "lder"], writes=[fk[4]])
                    S.op("act", lambda e: e.activation(out=F[5], in_=F[1], func=AF.Ln, scale=-1.0, bias=1.0),
                         reads=[fk[1]], writes=[fk[5]])
                    S.op("act", lambda e: e.activation(out=F[1], in_=F[5], func=AF.Exp, scale=0.5),
                         reads=[fk[5]], writes=[fk[1]])
                    S.op("pool", lambda e: e.tensor_tensor(out=F[2], in0=F[2], in1=xc[:, :], op=ALU.mult),
                         reads=[fk[2], "xc" + sfx], writes=[fk[2]])
                    S.op("dve", lambda e: e.tensor_tensor(out=F[3], in0=GB[:, 0:TB], in1=F[3], op=ALU.mult),
                         reads=[GK, fk[3]], writes=[fk[3]])

                def stageB(n):
                    c_ = ctx(n)
                    p, F, fk = c_["p"], c_["F"], c_["fk"]
                    S.op("pool", lambda e: e.tensor_tensor(out=F[2], in0=F[2], in1=F[1], op=ALU.mult),
                         reads=[fk[2], fk[1]], writes=[fk[2]])
                    S.op("dve", lambda e: e.tensor_tensor_scan(
                        out=F[5], data0=F[4], data1=F[2], initial=hst[:, n:n + 1], op0=ALU.mult, op1=ALU.add),
                        reads=[fk[4], fk[2], "hst%d" % n, "hst"], writes=[fk[5]])
                    S.op("dve", lambda e: e.tensor_copy(out=hst[:, n:n + 1], in_=f32t2[p][5][:, TB - 1:TB]),
                         reads=[fk[5]], writes=["hst%d" % n])
                    S.op("pool", lambda e: e.tensor_tensor(out=y1T[:, n, :], in0=F[5], in1=F[3], op=ALU.mult),
                         reads=[fk[5], fk[3]], writes=["y1T%d" % n])

                stageA0(0)
                stageA0(1)
                stageA1(0)
                stageA0(2)
                stageA1(1)
                stageA2(0)
                for n in range(8):
                    if n + 3 < 8:
                        stageA0(n + 3)
                    if n + 2 < 8:
                        stageA1(n + 2)
                    stageB(n)
                    if n + 1 < 8:
                        stageA2(n + 1)

            def stageO(cb):
                t0c = cb * TB
                q = cb % 2
                x1 = x1b[q]
                ssq = nrm["ssq"]
                y1k = ["y1T%d" % n for n in range(8)]
                if cb == 0:
                    scale_wo(1, wo1)
                for tt in range(2):
                    for hf in range(2):
                        bank = A[2 + hf]
                        bk = "A%d" % (2 + hf)
                        fns = []
                        for k in range(8):
                            fns.append(lambda e, k=k, tt=tt, hf=hf, bank=bank: e.matmul(
                                bank[:, :], lhsT=y1T[:, k, tt * 128:(tt + 1) * 128],
                                rhs=wo1[:, k, hf * 512:(hf + 1) * 512], start=(k == 0), stop=(k == 7)))
                        S.op("pe", fns, reads=y1k + ["wo1"], writes=[bk])
                        S.op("dve", lambda e, tt=tt, hf=hf, bank=bank: e.tensor_tensor(
                            out=x1[:, tt, hf * 512:(hf + 1) * 512], in0=bank[:, :],
                            in1=x1[:, tt, hf * 512:(hf + 1) * 512], op=ALU.add),
                            reads=[bk, "x1_%d_%d" % (q, tt)], writes=["x1_%d_%d" % (q, tt)])
                    oi = oti[0] % 2
                    oti[0] += 1
                    S.op("act", lambda e, tt=tt, oi=oi: e.activation(
                        out=outt[oi][:, :], in_=x1[:, tt, :], func=AF.Square, accum_out=ssq[:, 2 + tt:3 + tt]),
                        reads=["x1_%d_%d" % (q, tt)], writes=["outt%d" % oi, "ssqB%d" % (2 + tt)])
                    S.op("act", lambda e, tt=tt: e.activation(
                        out=nrm["lnv"][:, 2 + tt:3 + tt], in_=ssq[:, 2 + tt:3 + tt], func=AF.Ln,
                        scale=1.0 / D, bias=epsc[:, 0:1]), reads=["ssqB%d" % (2 + tt), "epsc"],
                        writes=["lnvB%d" % (2 + tt)])
                    S.op("act", lambda e, tt=tt: e.activation(
                        out=nrm["rstd"][:, 2 + tt:3 + tt], in_=nrm["lnv"][:, 2 + tt:3 + tt], func=AF.Exp,
                        scale=-0.5), reads=["lnvB%d" % (2 + tt)], writes=["rstdB%d" % (2 + tt)])
                    S.op("dve", lambda e, tt=tt, oi=oi: e.scalar_tensor_tensor(
                        out=outt[oi][:, :], in0=x1[:, tt, :], scalar=nrm["rstd"][:, 2 + tt:3 + tt], in1=gfbc[:, :],
                        op0=ALU.mult, op1=ALU.mult),
                        reads=["x1_%d_%d" % (q, tt), "rstdB%d" % (2 + tt), "gfbc"], writes=["outt%d" % oi])
                    S.dma("sp", out_d[b, t0c + tt * 128: t0c + (tt + 1) * 128, :], outt[oi][:, :],
                          reads=["outt%d" % oi])

            stageP(0)
            for cb in range(NTB):
                nloop(cb)
                if cb + 1 < NTB:
                    stageP(cb + 1)
                stageO(cb)
            S.barrier()
            AR.release(mB)

        S.wait_all("sp", S.all_events())
        S.emit()
        build.stats = dict(n_inst=S.n_inst, peak_sbuf=AR.peak,
                           per_eng={e: len(S.streams[e]) for e in S.ENG})
    return nc


_CACHE = {}


def make_in_maps(inputs, n_cores, nseq):
    kc, kf = host_consts()
    x = np.ascontiguousarray(inputs["x"], dtype=np.float32)
    c = np.asarray(inputs["c"], dtype=np.float32)
    pos = np.asarray(inputs["positions"]).astype(np.int32)
    lcols = np.zeros((128, 64), np.float32)
    cw = np.asarray(inputs["lru_conv_w"], np.float32)[0]
    for k in range(4):
        lcols[:, k * 8:(k + 1) * 8] = cw[k].reshape(8, 128).T
    for i, nm in enumerate(["lru_conv_b", "lru_b_a", "lru_b_x", "lru_lambda"]):
        lcols[:, 32 + i * 8: 40 + i * 8] = np.asarray(inputs[nm], np.float32)[0].reshape(8, 128).T
    shared = {
        "norm_g": np.asarray(inputs["norm_g"], np.float32),
        "w_mod": np.asarray(inputs["w_mod"], np.float32),
        "b_mod": np.asarray(inputs["b_mod"], np.float32),
        "attn_w_in": np.asarray(inputs["attn_w_in"], np.float32)[0],
        "attn_w_out": np.asarray(inputs["attn_w_out"], np.float32)[0],
        "lru_w_in": np.asarray(inputs["lru_w_in"], np.float32)[0],
        "lru_w_a": np.asarray(inputs["lru_w_a"], np.float32)[0],
        "lru_w_x": np.asarray(inputs["lru_w_x"], np.float32)[0],
        "lru_w_out": np.asarray(inputs["lru_w_out"], np.float32)[0],
        "lru_cols": lcols,
        "final_g": np.asarray(inputs["final_g"], np.float32).reshape(1, D),
        "kc": kc, "kf": kf,
    }
    maps = []
    for i in range(n_cores):
        sl = slice(i * nseq, (i + 1) * nseq)
        cc = c[sl]
        cT = np.ascontiguousarray(cc.reshape(nseq, 8, 128).transpose(2, 1, 0))
        m = dict(shared)
        m["x"] = x[sl]
        m["cT"] = cT
        m["pos"] = np.ascontiguousarray(pos[sl])
        maps.append(m)
    return maps


def kernel(**inputs):
    n_cores = 8
    B, S_LEN, _ = inputs["x"].shape
    nseq = B // n_cores
    key = (S_LEN, nseq)
    if key not in _CACHE:
        _CACHE[key] = build(S_LEN, nseq)
    nc = _CACHE[key]
    maps = make_in_maps(inputs, n_cores, nseq)
    res = run_bass_kernel_spmd(nc, maps, core_ids=list(range(n_cores)))
    out = np.concatenate([np.asarray(r["out"]) for r in res.results], axis=0)
    return out.astype(np.float32)
```
